# Optimizing a Trainium2 kernel written in Bass

```python
import jax, jax.numpy as jnp
from jax import lax
import numpy as np

D_MODEL = 1024
BATCH = 8
SEQ = 2048
DEPTH = 4

GRID_W = 64
CTX_LEN = 256
EPS = 1e-6

MIX_WIDTH = D_MODEL
FOURIER_WIDTH = MIX_WIDTH // 4
N_FOURIER_HEADS = 4
FOURIER_HEAD_DIM = FOURIER_WIDTH // N_FOURIER_HEADS
POOL_WIDTH = MIX_WIDTH // 4
POOL_WINDOWS = (2, 4, 8, 16)
POOL_GROUP_DIM = POOL_WIDTH // len(POOL_WINDOWS)
ATTN_WIDTH = MIX_WIDTH // 2
HEAD_DIM = 128
N_Q_HEADS = ATTN_WIDTH // HEAD_DIM
N_KV_HEADS = 2
Q_PER_KV = N_Q_HEADS // N_KV_HEADS
KV_WIDTH = N_KV_HEADS * HEAD_DIM
ROPE_HALF = HEAD_DIM // 2
ROPE_THETA = 10000.0
Q_BLOCK = 128

F_OFF = 0
P_OFF = F_OFF + FOURIER_WIDTH
Q_OFF = P_OFF + POOL_WIDTH
K_OFF = Q_OFF + ATTN_WIDTH
V_OFF = K_OFF + KV_WIDTH
IN_WIDTH = V_OFF + KV_WIDTH

N_EXPERTS = 16
EC_CAPACITY_FACTOR = 2
EXPERT_FF = D_MODEL

kernel_name = "hybrid_fourier_pool_gqa_ec_moe_dit"


def rms_norm(x, g):
    xf = x.astype(jnp.float32)
    y = xf * lax.rsqrt(jnp.mean(xf * xf, axis=-1, keepdims=True) + EPS)
    return (y * g.astype(jnp.float32)).astype(x.dtype)


def modulate(x, g, shift, scale):
    xf = x.astype(jnp.float32)
    y = xf * lax.rsqrt(jnp.mean(xf * xf, axis=-1, keepdims=True) + EPS) * g.astype(jnp.float32)
    y = y * (1.0 + scale.astype(jnp.float32)) + shift.astype(jnp.float32)
    return y.astype(x.dtype)


def axial_angles(n):
    rows = n // GRID_W
    row = jnp.repeat(jnp.arange(rows, dtype=jnp.float32), GRID_W)
    col = jnp.tile(jnp.arange(GRID_W, dtype=jnp.float32), rows)
    inv = ROPE_THETA ** (-jnp.arange(0, ROPE_HALF, 2, dtype=jnp.float32) / ROPE_HALF)
    return row[:, None] * inv[None, :], col[:, None] * inv[None, :]


def rope_half(x, ang):
    cos = jnp.cos(ang)[None, :, None, :]
    sin = jnp.sin(ang)[None, :, None, :]
    xf = x.astype(jnp.float32)
    x1, x2 = xf[..., : ROPE_HALF // 2], xf[..., ROPE_HALF // 2:]
    return jnp.concatenate([x1 * cos - x2 * sin, x2 * cos + x1 * sin], axis=-1).astype(x.dtype)


def axial_rope(x, ang_row, ang_col):
    return jnp.concatenate([rope_half(x[..., :ROPE_HALF], ang_row),
                            rope_half(x[..., ROPE_HALF:], ang_col)], axis=-1)


def attend(q, k, v):
    s = jnp.einsum('bqhgd,bkhd->bhgqk', q, k).astype(jnp.float32) * (HEAD_DIM ** -0.5)
    p = jax.nn.softmax(s, axis=-1).astype(v.dtype)
    return jnp.einsum('bhgqk,bkhd->bqhgd', p, v)


def blocked_attention(q, k, v):
    B, L = q.shape[0], q.shape[1]
    nb = L // Q_BLOCK
    qb = q.reshape(B, nb, Q_BLOCK, N_KV_HEADS, Q_PER_KV, HEAD_DIM).transpose(1, 0, 2, 3, 4, 5)
    ob = lax.map(lambda qi: attend(qi, k, v), qb)
    return ob.transpose(1, 0, 2, 3, 4, 5).reshape(B, L, ATTN_WIDTH)


def fourier_mix(f, w):
    B, n, _ = f.shape
    fh = f.astype(jnp.float32).reshape(B, n, N_FOURIER_HEADS, FOURIER_HEAD_DIM).transpose(0, 2, 1, 3)
    fr = jnp.real(jnp.fft.fft2(fh, axes=(-2, -1), norm="ortho"))
    fr = fr.transpose(0, 2, 1, 3).reshape(B, n, FOURIER_WIDTH).astype(f.dtype)
    return fr @ w


def pool_mix(p, w, scale):
    B, n, _ = p.shape
    pf = p.astype(jnp.float32)
    cs = jnp.concatenate([jnp.zeros((B, 1, POOL_WIDTH), jnp.float32), jnp.cumsum(pf, axis=1)], axis=1)
    t = jnp.arange(n)
    outs = []
    for gi, win in enumerate(POOL_WINDOWS):
        lo = jnp.clip(t - win // 2, 0, n)
        hi = jnp.clip(t + win // 2, 0, n)
        sl = slice(gi * POOL_GROUP_DIM, (gi + 1) * POOL_GROUP_DIM)
        csg = cs[..., sl]
        s = jnp.take(csg, hi, axis=1) - jnp.take(csg, lo, axis=1)
        cnt = (hi - lo).astype(jnp.float32)[None, :, None]
        outs.append(s / cnt - pf[..., sl])
    d = jnp.stack(outs, axis=2).astype(p.dtype)
    y = jnp.einsum('bngc,gcd->bngd', d, w).reshape(B, n, POOL_WIDTH)
    return y * scale


def expert_choice_ffn(h, w_router, w_gate, w_up, w_down):
    B, n, D = h.shape
    cap = EC_CAPACITY_FACTOR * n // N_EXPERTS
    logits = jnp.einsum('bnd,de->bne', h, w_router).astype(jnp.float32)
    aff = jax.nn.softmax(logits, axis=-1)
    g, idx = lax.top_k(aff.transpose(0, 2, 1), cap)
    xs = jax.vmap(lambda hb, ib: hb[ib])(h, idx)
    a = jnp.einsum('becd,edf->becf', xs, w_gate)
    u = jnp.einsum('becd,edf->becf', xs, w_up)
    y = jnp.einsum('becf,efd->becd', jax.nn.silu(a) * u, w_down)
    y = y * g[..., None].astype(y.dtype)
    return jax.vmap(lambda yb, ib: jnp.zeros((n, D), yb.dtype).at[ib.reshape(-1)].add(yb.reshape(-1, D)))(y, idx)


def setup_inputs(seed: int = 0) -> dict:
    key = jax.random.key(seed)
    ks = jax.random.split(key, 20)
    nrm = jax.random.normal
    f32 = jnp.float32
    D = D_MODEL
    return {
        "x": nrm(ks[0], (BATCH, SEQ, D), f32),
        "c": nrm(ks[1], (BATCH, D), f32),
        "ctx": nrm(ks[2], (BATCH, CTX_LEN, D), f32),
        "c_ctx": nrm(ks[3], (D,), f32),
        "ada_w": nrm(ks[4], (DEPTH, D, 6 * D), f32) * (0.5 * D ** -0.5),
        "ada_b": nrm(ks[5], (DEPTH, 6 * D), f32) * 0.02,
        "norm1_g": 1.0 + 0.02 * nrm(ks[6], (DEPTH, D), f32),
        "norm2_g": 1.0 + 0.02 * nrm(ks[7], (DEPTH, D), f32),
        "w_in": nrm(ks[8], (DEPTH, D, IN_WIDTH), f32) * D ** -0.5,
        "w_fourier": nrm(ks[9], (DEPTH, FOURIER_WIDTH, FOURIER_WIDTH), f32) * FOURIER_WIDTH ** -0.5,
        "w_pool": nrm(ks[10], (DEPTH, len(POOL_WINDOWS), POOL_GROUP_DIM, POOL_GROUP_DIM), f32) * POOL_GROUP_DIM ** -0.5,
        "pool_scale": 1.0 + 0.1 * nrm(ks[11], (DEPTH, POOL_WIDTH), f32),
        "q_norm_g": 1.0 + 0.02 * nrm(ks[12], (DEPTH, HEAD_DIM), f32),
        "k_norm_g": 1.0 + 0.02 * nrm(ks[13], (DEPTH, HEAD_DIM), f32),
        "w_out": nrm(ks[14], (DEPTH, MIX_WIDTH, D), f32) * MIX_WIDTH ** -0.5,
        "w_router": nrm(ks[15], (DEPTH, D, N_EXPERTS), f32) * D ** -0.5,
        "w_gate": nrm(ks[16], (DEPTH, N_EXPERTS, D, EXPERT_FF), f32) * D ** -0.5,
        "w_up": nrm(ks[17], (DEPTH, N_EXPERTS, D, EXPERT_FF), f32) * D ** -0.5,
        "w_down": nrm(ks[18], (DEPTH, N_EXPERTS, EXPERT_FF, D), f32) * EXPERT_FF ** -0.5,
    }


def reference(x, c, ctx, c_ctx, ada_w, ada_b, norm1_g, norm2_g, w_in, w_fourier, w_pool, pool_scale,
              q_norm_g, k_norm_g, w_out, w_router, w_gate, w_up, w_down):
    B, L, D = x.shape
    Lc = ctx.shape[1]
    ang_row, ang_col = axial_angles(L)
    sc = jax.nn.silu(c)
    scc = jax.nn.silu(c_ctx)
    for i in range(DEPTH):
        last = i == DEPTH - 1
        sh1, sc1, g1, sh2, sc2, g2 = jnp.split((sc @ ada_w[i] + ada_b[i])[:, None, :], 6, axis=-1)
        csh1, csc1, cg1, csh2, csc2, cg2 = jnp.split((scc @ ada_w[i] + ada_b[i])[None, None, :], 6, axis=-1)

        hx = modulate(x, norm1_g[i], sh1, sc1)
        hc = modulate(ctx, norm1_g[i], csh1, csc1)
        px_all = hx @ w_in[i]
        kvc = hc @ w_in[i][:, K_OFF:]
        kc = rms_norm(kvc[..., :KV_WIDTH].reshape(B, Lc, N_KV_HEADS, HEAD_DIM), k_norm_g[i])
        vc = kvc[..., KV_WIDTH:].reshape(B, Lc, N_KV_HEADS, HEAD_DIM)

        qx = rms_norm(px_all[..., Q_OFF:K_OFF].reshape(B, L, N_Q_HEADS, HEAD_DIM), q_norm_g[i])
        kx = rms_norm(px_all[..., K_OFF:V_OFF].reshape(B, L, N_KV_HEADS, HEAD_DIM), k_norm_g[i])
        vx = px_all[..., V_OFF:].reshape(B, L, N_KV_HEADS, HEAD_DIM)
        qx = axial_rope(qx, ang_row, ang_col)
        kx = axial_rope(kx, ang_row, ang_col)
        k_all = jnp.concatenate([kx, kc], axis=1)
        v_all = jnp.concatenate([vx, vc], axis=1)
        ax = blocked_attention(qx, k_all, v_all)
        ox = jnp.concatenate([fourier_mix(px_all[..., F_OFF:P_OFF], w_fourier[i]),
                              pool_mix(px_all[..., P_OFF:Q_OFF], w_pool[i], pool_scale[i]),
                              ax], axis=-1) @ w_out[i]
        x_new = x + g1 * ox

        x_new = x_new + g2 * expert_choice_ffn(modulate(x_new, norm2_g[i], sh2, sc2),
                                               w_router[i], w_gate[i], w_up[i], w_down[i])

        if not last:
            pc_all = hc @ w_in[i][:, :K_OFF]
            qc = rms_norm(pc_all[..., Q_OFF:K_OFF].reshape(B, Lc, N_KV_HEADS, Q_PER_KV, HEAD_DIM), q_norm_g[i])
            ac = attend(qc, kc, vc).reshape(B, Lc, ATTN_WIDTH)
            oc = jnp.concatenate([fourier_mix(pc_all[..., F_OFF:P_OFF], w_fourier[i]),
                                  pool_mix(pc_all[..., P_OFF:Q_OFF], w_pool[i], pool_scale[i]),
                                  ac], axis=-1) @ w_out[i]
            ctx = ctx + cg1 * oc
            ctx = ctx + cg2 * expert_choice_ffn(modulate(ctx, norm2_g[i], csh2, csc2),
                                                w_router[i], w_gate[i], w_up[i], w_down[i])
        x = x_new
    return x
```

```python
import math
from contextlib import ExitStack

import numpy as np
import ml_dtypes
import concourse.bass as bass
import concourse.mybir as mybir
from concourse.bass_utils import run_bass_kernel_spmd

F32 = mybir.dt.float32
BF16 = mybir.dt.bfloat16
U32 = mybir.dt.uint32
I32 = mybir.dt.int32
ALU = mybir.AluOpType
AF = mybir.ActivationFunctionType
AX = mybir.AxisListType
IOA = bass.IndirectOffsetOnAxis

D = 1024
L = 2048
LC = 256
NT = L + LC
NTILE = NT // 128
NE = 16
CAPX = 256
CAPC = 32
CAPT = CAPX + CAPC
EPS = 1e-6
SM_SHIFT = 12.0
NDMA_SEM = 6


class Prog:
    sems = None
    gcnt = None

    @classmethod
    def begin_program(cls, nc, stack):
        cls.sems = {}
        cls.gcnt = {}
        keys = ["pe", "act", "dve", "pool"] + ["dma_%s_%d" % (q, i) for q in ("sp", "act", "pool") for i in range(NDMA_SEM)]
        for k in keys:
            cls.sems[k] = stack.enter_context(nc.semaphore("s_" + k))
            cls.gcnt[k] = 0

    def __init__(self, nc, same_engine_sync=True):
        self.nc = nc
        self.same = same_engine_sync
        self.ops = {e: [] for e in ("pe", "act", "dve", "pool", "sp")}
        self.cidx = {e: 0 for e in self.ops}
        self.dcnt = Prog.gcnt
        self.known = {e: {} for e in self.ops}
        self.lastw = {}
        self.readers = {}
        self.dma_rr = {e: 0 for e in self.ops}
        self.miles = {e: set() for e in self.ops}

    def _deps(self, reads, writes):
        deps = []
        for r in reads:
            ev = self.lastw.get(r)
            if ev is not None:
                deps.append(ev)
        for w in writes:
            ev = self.lastw.get(w)
            if ev is not None:
                deps.append(ev)
            deps.extend(self.readers.get(w, ()))
        return deps

    def _commit(self, ev, reads, writes):
        for w in writes:
            self.lastw[w] = ev
            self.readers[w] = []
        for r in reads:
            if r in writes:
                continue
            self.readers.setdefault(r, []).append(ev)

    def _waits(self, eng, deps):
        need = {}
        for (kind, sk, val) in deps:
            if kind == "c" and sk == eng and (eng == "pe" or not self.same):
                continue
            key = (kind, sk)
            if self.known[eng].get(key, 0) >= val:
                continue
            if need.get(key, 0) < val:
                need[key] = val
        out = []
        for key, val in need.items():
            self.known[eng][key] = val
            if key[0] == "c":
                self.miles[key[1]].add(val)
            out.append((key[0], key[1], val))
        return out

    def op(self, eng, fn, reads=(), writes=()):
        reads = tuple(reads)
        writes = tuple(writes)
        waits = self._waits(eng, self._deps(reads, writes))
        self.cidx[eng] += 1
        ev = ("c", eng, self.cidx[eng])
        self.ops[eng].append((waits, fn, ev))
        self._commit(ev, reads, writes)
        return ev

    def dma(self, eng, fn, reads=(), writes=(), after=()):
        reads = tuple(reads)
        writes = tuple(writes)
        i = self.dma_rr[eng]
        self.dma_rr[eng] = (i + 1) % NDMA_SEM
        sk = "dma_%s_%d" % (eng, i)
        deps = self._deps(reads, writes) + [("d", sk, self.dcnt[sk])] + list(after)
        waits = self._waits(eng, deps)
        self.dcnt[sk] += 16
        ev = ("d", sk, self.dcnt[sk])
        self.ops[eng].append((waits, fn, ev))
        self._commit(ev, reads, writes)
        return ev

    def emit(self):
        nc = self.nc
        sems = Prog.sems
        final = [("c", e, self.cidx[e]) for e in ("pe", "act", "dve", "pool") if self.cidx[e] > 0]
        final += [("d", k, v) for k, v in self.dcnt.items() if k.startswith("dma_") and v > 0]
        waits = self._waits("sp", final)
        self.ops["sp"].append((waits, None, None))
        cmap = {}
        for e in ("pe", "act", "dve", "pool"):
            for rank, idx in enumerate(sorted(self.miles[e])):
                cmap[(e, idx)] = Prog.gcnt[e] + rank + 1
            Prog.gcnt[e] += len(self.miles[e])
        with nc.Block() as block:
            engmap = {"pe": block.tensor, "act": block.scalar, "dve": block.vector,
                      "pool": block.gpsimd, "sp": block.sync}
            for e, lst in self.ops.items():
                if not lst:
                    continue

                def body(engine, lst=lst):
                    for w, fn, ev in lst:
                        for (kind, sk, val) in w:
                            if kind == "c":
                                engine.wait_ge(sems[sk], cmap[(sk, val)])
                            else:
                                engine.wait_ge(sems[sk], val)
                        if fn is None:
                            continue
                        ins = fn(engine)
                        if ev[0] == "d":
                            ins.then_inc(sems[ev[1]], 16)
                        elif (ev[1], ev[2]) in cmap:
                            ins.then_inc(sems[ev[1]], 1)
                engmap[e](body)


def _bf(a):
    return np.ascontiguousarray(a.astype(np.float32)).astype(ml_dtypes.bfloat16)


def make_consts():
    c = {}
    c["ident_bf"] = _bf(np.eye(128))
    c["ident_f"] = np.eye(128, dtype=np.float32)
    c["onesm_bf"] = _bf(np.full((128, 128), 1.0 / 128.0))
    c["ones_bf"] = _bf(np.ones((128, 128)))
    d = np.arange(128)
    partner = np.where(d % 64 < 32, d + 32, d - 32)
    perm = np.zeros((128, 128), np.float32)
    perm[partner, d] = 1.0
    c["perm_f"] = perm
    t = np.arange(L)
    inv = 10000.0 ** (-(np.arange(0, 64, 2, dtype=np.float64)) / 64.0)
    pos = np.where((d // 64)[:, None] == 0, (t // 64)[None, :], (t % 64)[None, :]).astype(np.float64)
    ang = (pos.astype(np.float32) * inv.astype(np.float32)[d % 32][:, None]).astype(np.float64)
    sgn = np.where(d % 64 < 32, -1.0, 1.0)[:, None]
    c["cosT"] = np.cos(ang).astype(np.float32)
    c["sinT"] = (sgn * np.sin(ang)).astype(np.float32)
    i = np.arange(256)
    same = (i[:, None] // 64) == (i[None, :] // 64)
    ph = 2 * np.pi * ((i[:, None] % 64) * (i[None, :] % 64) % 64) / 64.0
    bdc = np.where(same, np.cos(ph), 0.0) / 8.0
    bds = np.where(same, -np.sin(ph), 0.0) / 8.0
    bd = np.concatenate([bdc, bds], axis=1)
    c["bd"] = _bf(bd.reshape(2, 128, 512).transpose(1, 0, 2))
    for n, nm in ((L, "x"), (LC, "c")):
        k = np.arange(n)
        ph = 2 * np.pi * ((k[:, None] * k[None, :]) % n) / n
        c["cn_" + nm] = _bf(np.cos(ph) / math.sqrt(n))
        c["sn_" + nm] = _bf(np.sin(ph) / math.sqrt(n))
    p = np.arange(128)
    wins = np.array([[2, 4], [8, 16]])
    invw = np.zeros((128, 2), np.float32)
    invb = np.zeros((128, 2, 16), np.float32)
    for ch in range(2):
        for pp in range(128):
            w = wins[ch, pp // 64]
            invw[pp, ch] = 1.0 / w
            for ii in range(8):
                tt = ii
                cnt = min(tt + w // 2, 1 << 30) - max(tt - w // 2, 0)
                invb[pp, ch, ii] = 1.0 / cnt
                r = 7 - ii
                cnt = (w // 2 if r >= w // 2 else r + 1) + w // 2
                cnt = min(w // 2, r + 1) + w // 2
                invb[pp, ch, 8 + ii] = 1.0 / cnt
    c["invw"] = invw
    c["invb"] = invb
    return c


CONST_SPECS = [
    ("ident_bf", [128, 128], BF16), ("ident_f", [128, 128], F32), ("onesm_bf", [128, 128], BF16),
    ("ones_bf", [128, 128], BF16), ("perm_f", [128, 128], F32), ("cosT", [128, L], F32),
    ("sinT", [128, L], F32), ("bd", [128, 2, 512], BF16), ("cn_x", [L, L], BF16), ("sn_x", [L, L], BF16),
    ("cn_c", [LC, LC], BF16), ("sn_c", [LC, LC], BF16), ("invw", [128, 2], F32), ("invb", [128, 2, 16], F32),
]


def build_program(depth, dbg=()):
    nc = bass.Bass("TRN2", target_bir_lowering=False)
    dt_in = {}

    def inp(name, shape, dt=F32):
        dt_in[name] = nc.dram_tensor(name, list(shape), dt, kind="ExternalInput").ap()
        return dt_in[name]

    x_in = inp("x", [L, D])
    ctx_in = inp("ctx", [LC, D])
    cT_in = inp("cT", [128, 8, 2])
    ada_w = inp("ada_w", [depth, D, 6 * D])
    ada_b = inp("ada_b", [depth, 6 * D])
    n1g = inp("norm1_g", [depth, D])
    n2g = inp("norm2_g", [depth, D])
    w_in = inp("w_in", [depth, D, 1536])
    w_f = inp("w_fourier", [depth, 256, 256])
    w_p = inp("w_pool", [depth, 4, 64, 64])
    pscale = inp("pool_scale", [depth, 128, 2])
    qg = inp("q_norm_g", [depth, 128])
    kg = inp("k_norm_g", [depth, 128])
    w_out = inp("w_out", [depth, D, D])
    w_r = inp("w_router", [depth, D, NE])
    with_experts = not any(d.startswith("stop_") for d in dbg)
    w_g = inp("w_gate", [depth, NE, D, D]) if with_experts else None
    w_u = inp("w_up", [depth, NE, D, D]) if with_experts else None
    w_d = inp("w_down", [depth, NE, D, D]) if with_experts else None
    cst = {nm: inp("k_" + nm, shp, dt) for nm, shp, dt in CONST_SPECS}
    out = nc.dram_tensor("out", [L, D], F32, kind="ExternalOutput").ap()
    dbg_out = {}

    def dram(name, shape, dt):
        if name in dbg:
            dbg_out[name] = nc.dram_tensor(name, list(shape), dt, kind="ExternalOutput").ap()
            return dbg_out[name]
        return nc.dram_tensor(name, list(shape), dt, kind="Internal").ap()

    X = dram("X", [NT, D], F32)
    MOD = dram("MOD", [depth, 2, 6 * D], F32)
    H2 = dram("H2", [NT, D], BF16)

    top = ExitStack()
    with top:
        uid = [0]

        def sb(name, shape, dt, st=top):
            uid[0] += 1
            return st.enter_context(nc.sbuf_tensor("%s_%d" % (name, uid[0]), list(shape), dt))

        def ps(name, shape, dt=F32, st=top):
            uid[0] += 1
            return st.enter_context(nc.psum_tensor("%s_%d" % (name, uid[0]), list(shape), dt))

        Prog.begin_program(nc, top)
        ident_bf = sb("ident_bf", [128, 128], BF16)
        ident_f = sb("ident_f", [128, 128], F32)
        onesm_bf = sb("onesm_bf", [128, 128], BF16)
        ones_bf = sb("ones_bf", [128, 128], BF16)
        perm_f = sb("perm_f", [128, 128], F32)
        cosT = sb("cosT", [128, L], F32)
        sinT = sb("sinT", [128, L], F32)
        bd = sb("bd", [128, 2, 512], BF16)
        cn_c = sb("cn_c", [128, 2, LC], BF16)
        sn_c = sb("sn_c", [128, 2, LC], BF16)
        invw = sb("invw", [128, 2], F32)
        invb = sb("invb", [128, 2, 16], F32)
        scT = sb("scT", [128, 8, 2], BF16)
        IDXP = sb("IDXP", [128, NE, 3], U32)
        AFF = sb("AFF", [128, NTILE, NE], F32)
        GP = sb("GP", [128, NE, 3], F32)

        with ExitStack() as st:
            P = Prog(nc)
            cTf = sb("cTf", [128, 8, 2], F32, st)
            for nm, t in (("ident_bf", ident_bf), ("ident_f", ident_f), ("onesm_bf", onesm_bf), ("ones_bf", ones_bf),
                          ("perm_f", perm_f), ("cosT", cosT), ("sinT", sinT), ("bd", bd), ("invw", invw), ("invb", invb)):
                P.dma("sp", lambda e, t=t, nm=nm: e.dma_start(out=t[:], in_=cst[nm]), writes=[nm])
            P.dma("sp", lambda e: e.dma_start(out=cn_c[:], in_=cst["cn_c"].rearrange("(t p) k -> p t k", p=128)), writes=["cn_c"])
            P.dma("sp", lambda e: e.dma_start(out=sn_c[:], in_=cst["sn_c"].rearrange("(t p) k -> p t k", p=128)), writes=["sn_c"])
            P.dma("sp", lambda e: e.dma_start(out=X[0:L, :], in_=x_in), writes=["X"])
            P.dma("sp", lambda e: e.dma_start(out=X[L:NT, :], in_=ctx_in), writes=["X"])
            P.op("pool", lambda e: e.memset(IDXP[:], 0), writes=["IDXP"])
            P.op("pool", lambda e: e.memset(GP[:], 0.0), writes=["GP"])
            P.dma("sp", lambda e: e.dma_start(out=cTf[:], in_=cT_in), writes=["cTf"])
            P.op("act", lambda e: e.activation(out=scT[:], in_=cTf[:], func=AF.Silu), reads=["cTf"], writes=["scT"])
            emit_ada(nc, P, st, sb, ps, 0, scT, ada_w, ada_b, MOD)
            P.emit()

        if "stop_init" in dbg:
            depth_run = 0
        else:
            depth_run = depth

        C = {"ident_bf": ident_bf, "ident_f": ident_f, "onesm_bf": onesm_bf, "ones_bf": ones_bf, "perm_f": perm_f,
             "cosT": cosT, "sinT": sinT, "bd": bd, "cn_c": cn_c, "sn_c": sn_c, "invw": invw, "invb": invb}
        stop = [d for d in dbg if d.startswith("stop_")]
        stop = stop[0] if stop else None
        for l in range(depth_run):
            last = (l == depth - 1) and ("notlast" not in dbg)
            done = False
            with ExitStack() as ast:
                bigA = sb("bigA", [128, 8, NT], BF16, ast)
                with ExitStack() as st:
                    P = Prog(nc)
                    emit_modulate(nc, P, st, sb, ps, l, X, MOD, n1g, ident_bf, bigA, 0, None, NTILE)
                    P.emit()
                if stop == "stop_1a":
                    done = True
                if not done:
                    with ExitStack() as pst:
                        fT = sb("fT", [128, 2, NT], BF16, pst)
                        pT = sb("pT", [128, 2, NT], F32, pst)
                        qT = sb("qT", [128, 4, NT], BF16, pst)
                        kT = sb("kT", [128, 2, NT], BF16, pst)
                        vv = sb("vv", [128, NTILE, 256], BF16, pst)
                        with ExitStack() as st:
                            P = Prog(nc)
                            emit_proj(nc, P, st, sb, ps, l, C, w_in, qg, kg, bigA, fT, pT, qT, kT, vv, last)
                            P.emit()
                        if stop == "stop_1b":
                            with ExitStack() as st:
                                P = Prog(nc)
                                for nm, t, shp, dt in (("fT", fT, [128, 2, NT], BF16), ("pT", pT, [128, 2, NT], F32), ("qT", qT, [128, 4, NT], BF16),
                                                       ("kT", kT, [128, 2, NT], BF16), ("vv", vv, [128, NTILE, 256], BF16)):
                                    dd = nc.dram_tensor("d_" + nm, shp, dt, kind="ExternalOutput").ap()
                                    dbg_out["d_" + nm] = dd
                                    P.dma("sp", lambda e, dd=dd, t=t: e.dma_start(out=dd, in_=t[:]), writes=["d_" + nm])
                                P.emit()
                            done = True
                        else:
                            with ExitStack() as st:
                                P = Prog(nc)
                                emit_fourier(nc, P, st, sb, ps, l, C, cst, w_f, fT, bigA, last)
                                P.emit()
                            with ExitStack() as st:
                                P = Prog(nc)
                                if l + 1 < depth:
                                    emit_ada(nc, P, st, sb, ps, l + 1, scT, ada_w, ada_b, MOD)
                                emit_pool(nc, P, st, sb, ps, l, C, w_p, pscale, pT, bigA, last)
                                P.emit()
                            with ExitStack() as st:
                                P = Prog(nc)
                                emit_attn(nc, P, st, sb, ps, l, C, qT, kT, vv, bigA, last)
                                P.emit()
                if stop == "stop_1c":
                    done = True
                if not done:
                    with ExitStack() as st:
                        P = Prog(nc)
                        emit_outproj(nc, P, st, sb, ps, l, C, X, MOD, n2g, w_out, bigA, bigA, H2, last)
                        P.emit()
                    if stop == "stop_1d":
                        done = True
                if not done:
                    with ExitStack() as st:
                        P = Prog(nc)
                        emit_router_logits(nc, P, st, sb, ps, l, C, w_r, bigA, AFF, last)
                        P.emit()
                    if stop == "stop_1e":
                        done = True
                if done and ("hT" in dbg):
                    with ExitStack() as st:
                        P = Prog(nc)
                        hT_d = nc.dram_tensor("hT", [128, 8, NT], BF16, kind="ExternalOutput").ap()
                        dbg_out["hT"] = hT_d
                        P.dma("sp", lambda e: e.dma_start(out=hT_d, in_=bigA[:]), writes=["hT_d"])
                        P.emit()
            if done:
                break
            with ExitStack() as st:
                P = Prog(nc)
                emit_experts(nc, P, st, sb, ps, l, C, MOD, w_g, w_u, w_d, H2, X, IDXP, GP, last, AFF)
                P.emit()

        with ExitStack() as st:
            P = Prog(nc)
            P.dma("sp", lambda e: e.dma_start(out=out, in_=X[0:L, :]), writes=["out"])
            if "IDXP" in dbg:
                i_d = nc.dram_tensor("d_IDXP", [128, NE, 3], U32, kind="ExternalOutput").ap()
                g_d = nc.dram_tensor("d_GP", [128, NE, 3], F32, kind="ExternalOutput").ap()
                dbg_out["d_IDXP"] = i_d
                dbg_out["d_GP"] = g_d
                P.dma("sp", lambda e: e.dma_start(out=i_d, in_=IDXP[:]), writes=["i_d"])
                P.dma("sp", lambda e: e.dma_start(out=g_d, in_=GP[:]), writes=["g_d"])
            P.emit()
    return nc, list(dbg_out.keys())


def run_pipeline(n, stages, skews):
    for step in range(n + max(skews)):
        for stg, sk in zip(stages, skews):
            i = step - sk
            if 0 <= i < n:
                stg(i)


def emit_ada(nc, P, st, sb, ps, l, scT, ada_w, ada_b, MOD):
    wada = [sb("wada%d" % i, [128, 8, 512], BF16, st) for i in range(2)]
    adab = [sb("adab%d" % i, [2, 512], F32, st) for i in range(2)]
    mrow = [sb("mrow%d" % i, [2, 512], F32, st) for i in range(2)]
    pmod = [ps("pmod%d" % i, [2, 512], F32, st) for i in range(2)]
    awv = ada_w[l].rearrange("(k p) n -> p k n", p=128)
    for cb in range(12):
        b = cb % 2
        P.dma("pool", lambda e, b=b, cb=cb: e.dma_start(out=wada[b][:], in_=awv[:, :, cb * 512:(cb + 1) * 512]), writes=[("wada", b)])
        for r in range(2):
            P.dma("sp", lambda e, b=b, cb=cb, r=r: e.dma_start(out=adab[b][r:r + 1, :], in_=ada_b[l:l + 1, cb * 512:(cb + 1) * 512]),
                  writes=[("adab", b, r)])
        for k in range(8):
            P.op("pe", lambda e, b=b, k=k: e.matmul(pmod[b][:], lhsT=scT[:, k, :], rhs=wada[b][:, k, :], start=(k == 0), stop=(k == 7)),
                 reads=["scT", ("wada", b)], writes=[("pmod", b)])
        P.op("act", lambda e, b=b: e.activation(out=mrow[b][:], in_=pmod[b][:], func=AF.Copy), reads=[("pmod", b)], writes=[("mrow", b)])
        P.op("pool", lambda e, b=b: e.tensor_tensor(out=mrow[b][:], in0=mrow[b][:], in1=adab[b][:], op=ALU.add),
             reads=[("mrow", b), ("adab", b, 0), ("adab", b, 1)], writes=[("mrow", b)])
        P.dma("sp", lambda e, b=b, cb=cb: e.dma_start(out=MOD[l, :, cb * 512:(cb + 1) * 512], in_=mrow[b][:]), reads=[("mrow", b)],
              writes=[("modout", cb)])


def bcast_rows(ap_row):
    return ap_row.partition_broadcast(128)


def emit_modulate(nc, P, st, sb, ps, l, X, MOD, ng, ident_bf, dstT, which, H2, ntiles, load_fn=None, dst_key=None):
    sh_off = 0 if which == 0 else 3 * D
    sc_off = sh_off + D
    NBF = 3
    gnb = sb("gnb", [128, D], F32, st)
    gm = [sb("gm%d" % r, [128, D], F32, st) for r in range(2)]
    sh = [sb("sh%d" % r, [128, D], F32, st) for r in range(2)]
    xt = [sb("xt%d" % i, [128, D], F32, st) for i in range(NBF)]
    y1 = [sb("y1_%d" % i, [128, D], F32, st) for i in range(2)]
    hb = [sb("hb%d" % i, [128, D], BF16, st) for i in range(NBF)]
    junk = sb("junk", [128, D], BF16, st)
    ss = sb("ss", [128, NTILE], F32, st)
    rstd = sb("rstd", [128, NTILE], F32, st)
    ptr = [ps("ptr%d" % i, [128, 8, 128], BF16, st) for i in range(2)]
    P.dma("sp", lambda e: e.dma_start(out=gnb[:], in_=bcast_rows(ng[l])), writes=["gnb"])
    for r in range(2):
        P.dma("sp", lambda e, r=r: e.dma_start(out=gm[r][:], in_=bcast_rows(MOD[l, r, sc_off:sc_off + D])), writes=[("gm", r)])
        P.dma("sp", lambda e, r=r: e.dma_start(out=sh[r][:], in_=bcast_rows(MOD[l, r, sh_off:sh_off + D])), writes=[("sh", r)])
        P.op("dve", lambda e, r=r: e.scalar_tensor_tensor(out=gm[r][:], in0=gm[r][:], scalar=1.0, in1=gnb[:], op0=ALU.add, op1=ALU.mult),
             reads=["gnb", ("gm", r)], writes=[("gm", r)])

    def stage_a1(t):
        b = t % NBF
        if load_fn is None:
            P.dma("sp", lambda e, b=b, t=t: e.dma_start(out=xt[b][:], in_=X[t * 128:(t + 1) * 128, :]), reads=["X"], writes=[("xt", b)])
        else:
            load_fn(t, b, xt[b])

    def stage_a2(t):
        b = t % NBF
        yb = t % 2
        r = 0 if t < 16 else 1
        P.op("act", lambda e, b=b, t=t: e.activation(out=junk[:], in_=xt[b][:], func=AF.Square, accum_out=ss[:, t:t + 1]),
             reads=[("xt", b)], writes=["junk", ("ss", t)])
        P.op("act", lambda e, t=t: e.activation(out=rstd[:, t:t + 1], in_=ss[:, t:t + 1], func=AF.Sqrt, scale=1.0 / D, bias=EPS),
             reads=[("ss", t)], writes=[("rstd", t)])
        P.op("dve", lambda e, t=t: e.reciprocal(out=rstd[:, t:t + 1], in_=rstd[:, t:t + 1]), reads=[("rstd", t)], writes=[("rstd", t)])
        P.op("dve", lambda e, b=b, yb=yb, t=t, r=r: e.scalar_tensor_tensor(out=y1[yb][:], in0=xt[b][:], scalar=rstd[:, t:t + 1], in1=gm[r][:],
                                                                        op0=ALU.mult, op1=ALU.mult),
             reads=[("xt", b), ("rstd", t), ("gm", r)], writes=[("y1", yb)])
        P.op("pool", lambda e, b=b, yb=yb, r=r: e.tensor_tensor(out=hb[b][:], in0=y1[yb][:], in1=sh[r][:], op=ALU.add),
             reads=[("y1", yb), ("sh", r)], writes=[("hb", b)])
        if H2 is not None:
            P.dma("sp", lambda e, b=b, t=t: e.dma_start(out=H2[t * 128:(t + 1) * 128, :], in_=hb[b][:]), reads=[("hb", b)], writes=[("H2", t)])

    def stage_b(t):
        b = t % NBF
        pb = t % 2
        for k in range(8):
            P.op("pe", lambda e, b=b, pb=pb, k=k: e.transpose(out=ptr[pb][:, k, :], in_=hb[b][:, k * 128:(k + 1) * 128], identity=ident_bf[:]),
                 reads=[("hb", b)], writes=[("ptr", pb)])
        P.op("act", lambda e, pb=pb, t=t: e.activation(out=dstT[:, :, t * 128:(t + 1) * 128], in_=ptr[pb][:], func=AF.Copy),
             reads=[("ptr", pb)], writes=[dst_key(t) if dst_key else ("dstT", t)])

    run_pipeline(ntiles, [stage_a1, stage_a2, stage_b], [0, 1, 2] if load_fn is None else [0, 1, 3])


TB = [(0, 512), (512, 512), (1024, 512), (1536, 512), (2048, 256)]


def emit_proj(nc, P, st, sb, ps, l, C, w_in, qg, kg, hT, fT, pT, qT, kT, v, last):
    wi = sb("wi", [128, 8, 1536], BF16, st)
    gq = sb("gq", [128, 1], F32, st)
    gk = sb("gk", [128, 1], F32, st)
    sq = [sb("sq%d" % i, [128, 512], BF16, st) for i in range(2)]
    rs = [sb("rs%d" % i, [128, 512], F32, st) for i in range(2)]
    qn = [sb("qn%d" % i, [128, 512], F32, st) for i in range(3)]
    t1 = [sb("t1%d" % i, [128, 512], F32, st) for i in range(2)]
    t2 = [sb("t2%d" % i, [128, 512], F32, st) for i in range(2)]
    pp = [ps("pp%d" % i, [128, 512], F32, st) for i in range(3)]
    pss = [ps("pss%d" % i, [128, 512], F32, st) for i in range(2)]
    prot = [ps("prot%d" % i, [128, 512], F32, st) for i in range(2)]
    wv = w_in[l].rearrange("(k p) n -> p k n", p=128)
    for h in range(4):
        P.dma("pool", lambda e, h=h: e.dma_start(out=wi[:, 2 * h:2 * h + 2, :], in_=wv[:, 2 * h:2 * h + 2, :]), writes=[("wi", h)])
    WI = [("wi", h) for h in range(4)]
    P.dma("sp", lambda e: e.dma_start(out=gq[:], in_=qg[l].rearrange("(p o) -> p o", o=1)), writes=["gq"])
    P.dma("sp", lambda e: e.dma_start(out=gk[:], in_=kg[l].rearrange("(p o) -> p o", o=1)), writes=["gk"])
    items = [(cc, t0, n) for cc in range(4, 10) for (t0, n) in TB] + [(cc, t0, n) for cc in range(4) for (t0, n) in TB]
    for t in range(NTILE):
        items.append(("v", t, 256))

    def s0(i):
        cc, t0, n = items[i]
        b = i % 3
        if cc == "v":
            t = t0
            for k in range(8):
                P.op("pe", lambda e, b=b, k=k, t=t: e.matmul(pp[b][:, :256], lhsT=hT[:, k, t * 128:(t + 1) * 128], rhs=wi[:, k, 1280:1536],
                                                            start=(k == 0), stop=(k == 7)),
                     reads=WI + ["hT"], writes=[("pp", b)])
            P.op("act", lambda e, b=b, t=t: e.activation(out=v[:, t, :], in_=pp[b][:, :256], func=AF.Copy), reads=[("pp", b)], writes=["v"])
            return
        for k in range(8):
            P.op("pe", lambda e, b=b, k=k, cc=cc, t0=t0, n=n: e.matmul(pp[b][:, :n], lhsT=wi[:, k, cc * 128:(cc + 1) * 128],
                                                                      rhs=hT[:, k, t0:t0 + n], start=(k == 0), stop=(k == 7)),
                 reads=WI + ["hT"], writes=[("pp", b)])
        if cc < 2:
            P.op("act", lambda e, b=b, cc=cc, t0=t0, n=n: e.activation(out=fT[:, cc, t0:t0 + n], in_=pp[b][:, :n], func=AF.Copy),
                 reads=[("pp", b)], writes=["fT"])
        elif cc < 4:
            P.op("act", lambda e, b=b, cc=cc, t0=t0, n=n: e.activation(out=pT[:, cc - 2, t0:t0 + n], in_=pp[b][:, :n], func=AF.Copy),
                 reads=[("pp", b)], writes=["pT"])
        else:
            P.op("act", lambda e, b=b, n=n, i=i: e.activation(out=sq[i % 2][:, :n], in_=pp[b][:, :n], func=AF.Square),
                 reads=[("pp", b)], writes=[("sq", i % 2)])

    def s1(i):
        cc, t0, n = items[i]
        if cc == "v" or cc < 4:
            return
        b = i % 3
        b2 = i % 2
        isq = cc < 8
        gv = gq if isq else gk
        gname = "gq" if isq else "gk"
        P.op("pe", lambda e, b2=b2, n=n: e.matmul(pss[b2][:, :n], lhsT=C["onesm_bf"][:], rhs=sq[b2][:, :n], start=True, stop=True),
             reads=[("sq", b2)], writes=[("pss", b2)])
        P.op("act", lambda e, b2=b2, n=n: e.activation(out=rs[b2][:, :n], in_=pss[b2][:, :n], func=AF.Sqrt, bias=EPS),
             reads=[("pss", b2)], writes=[("rs", b2)])
        P.op("dve", lambda e, b2=b2, n=n: e.reciprocal(out=rs[b2][:, :n], in_=rs[b2][:, :n]), reads=[("rs", b2)], writes=[("rs", b2)])
        P.op("dve", lambda e, b=b, b2=b2, n=n, gv=gv: e.scalar_tensor_tensor(out=qn[b][:, :n], in0=pp[b][:, :n], scalar=gv[:, 0:1],
                                                                          in1=rs[b2][:, :n], op0=ALU.mult, op1=ALU.mult),
             reads=[("pp", b), ("rs", b2), gname], writes=[("qn", b)])

    def s2(i):
        cc, t0, n = items[i]
        if cc == "v" or cc < 4:
            return
        b = i % 3
        b2 = i % 2
        isq = cc < 8
        dst = qT[:, cc - 4, t0:t0 + n] if isq else kT[:, cc - 8, t0:t0 + n]
        dname = "qT" if isq else "kT"
        if t0 >= L:
            P.op("pool", lambda e, b=b, n=n, dst=dst: e.tensor_copy(out=dst, in_=qn[b][:, :n]), reads=[("qn", b)], writes=[dname])
            return
        P.op("pe", lambda e, b=b, b2=b2, n=n: e.matmul(prot[b2][:, :n], lhsT=C["perm_f"][:], rhs=qn[b][:, :n], start=True, stop=True),
             reads=[("qn", b)], writes=[("prot", b2)])
        P.op("pool", lambda e, b=b, b2=b2, n=n, t0=t0: e.tensor_tensor(out=t1[b2][:, :n], in0=qn[b][:, :n], in1=C["cosT"][:, t0:t0 + n], op=ALU.mult),
             reads=[("qn", b)], writes=[("t1", b2)])
        P.op("dve", lambda e, b2=b2, n=n, t0=t0: e.tensor_tensor(out=t2[b2][:, :n], in0=prot[b2][:, :n], in1=C["sinT"][:, t0:t0 + n], op=ALU.mult),
             reads=[("prot", b2)], writes=[("t2", b2)])
        P.op("pool", lambda e, b2=b2, n=n, dst=dst: e.tensor_tensor(out=dst, in0=t1[b2][:, :n], in1=t2[b2][:, :n], op=ALU.add),
             reads=[("t1", b2), ("t2", b2)], writes=[dname])

    run_pipeline(len(items), [s0, s1, s2], [0, 1, 2])


def emit_fourier(nc, P, st, sb, ps, l, C, cst, w_f, fT, catT, last):
    g = sb("g", [128, NTILE, 512], BF16, st)
    frT = sb("frT", [128, 2, NT], BF16, st)
    wf = sb("wf", [128, 2, 256], BF16, st)
    tb = [[sb("tb%d_%d" % (i, j), [128, 8, 512], BF16, st) for j in range(2)] for i in range(2)]
    pg = [ps("pg%d" % i, [128, 512], F32, st) for i in range(2)]
    pf = [ps("pf%d" % i, [128, 512], F32, st) for i in range(2)]
    po = [ps("pfo%d" % i, [128, 512], F32, st) for i in range(2)]
    P.dma("pool", lambda e: e.dma_start(out=wf[:], in_=w_f[l].rearrange("(k p) n -> p k n", p=128)), writes=["wf"])
    ntl = 16 if last else NTILE
    for t in range(ntl):
        b = t % 2
        for c in range(2):
            P.op("pe", lambda e, b=b, c=c, t=t: e.matmul(pg[b][:], lhsT=fT[:, c, t * 128:(t + 1) * 128], rhs=C["bd"][:, c, :],
                                                        start=(c == 0), stop=(c == 1)),
                 reads=["fT"], writes=[("pg", b)])
        P.op("act", lambda e, b=b, t=t: e.activation(out=g[:, t, :], in_=pg[b][:], func=AF.Copy), reads=[("pg", b)], writes=[("g", t)])
    cnv = cst["cn_x"].rearrange("(t p) k -> p t k", p=128)
    snv = cst["sn_x"].rearrange("(t p) k -> p t k", p=128)
    it = 0
    for kb in range(4):
        for h in range(2):
            bb = it % 2
            it += 1
            P.dma("sp", lambda e, bb=bb, h=h, kb=kb: e.dma_start(out=tb[bb][0][:], in_=cnv[:, 8 * h:8 * h + 8, kb * 512:(kb + 1) * 512]),
                  writes=[("tb", bb, 0)])
            P.dma("sp", lambda e, bb=bb, h=h, kb=kb: e.dma_start(out=tb[bb][1][:], in_=snv[:, 8 * h:8 * h + 8, kb * 512:(kb + 1) * 512]),
                  writes=[("tb", bb, 1)])
            for c in range(2):
                for tt in range(8):
                    t = 8 * h + tt
                    for j in range(2):
                        first = (h == 0 and tt == 0 and j == 0)
                        lastmm = (h == 1 and tt == 7 and j == 1)
                        P.op("pe", lambda e, bb=bb, c=c, t=t, tt=tt, j=j, first=first, lastmm=lastmm:
                             e.matmul(pf[c][:], lhsT=g[:, t, j * 256 + c * 128: j * 256 + (c + 1) * 128], rhs=tb[bb][j][:, tt, :],
                                      start=first, stop=lastmm),
                             reads=[("g", t), ("tb", bb, j)], writes=[("pf", c)])
        for c in range(2):
            P.op("act", lambda e, c=c, kb=kb: e.activation(out=frT[:, c, kb * 512:(kb + 1) * 512], in_=pf[c][:], func=AF.Copy),
                 reads=[("pf", c)], writes=["frT"])
    if not last:
        for c in range(2):
            for tt in range(2):
                for j in range(2):
                    tabl = C["cn_c"] if j == 0 else C["sn_c"]
                    P.op("pe", lambda e, c=c, tt=tt, j=j, tabl=tabl:
                         e.matmul(pf[c][:, :LC], lhsT=g[:, 16 + tt, j * 256 + c * 128: j * 256 + (c + 1) * 128], rhs=tabl[:, tt, :],
                                  start=(tt == 0 and j == 0), stop=(tt == 1 and j == 1)),
                         reads=[("g", 16 + tt)], writes=[("pf", c)])
            P.op("act", lambda e, c=c: e.activation(out=frT[:, c, L:NT], in_=pf[c][:, :LC], func=AF.Copy), reads=[("pf", c)], writes=["frT"])
    it = 0
    for co in range(2):
        for (t0, n) in (TB[:4] if last else TB):
            b = it % 2
            it += 1
            for ci in range(2):
                P.op("pe", lambda e, b=b, co=co, ci=ci, t0=t0, n=n: e.matmul(po[b][:, :n], lhsT=wf[:, ci, co * 128:(co + 1) * 128],
                                                                            rhs=frT[:, ci, t0:t0 + n], start=(ci == 0), stop=(ci == 1)),
                     reads=["wf", "frT"], writes=[("po", b)])
            P.op("act", lambda e, b=b, co=co, t0=t0, n=n: e.activation(out=catT[:, co, t0:t0 + n], in_=po[b][:, :n], func=AF.Copy),
                 reads=[("po", b)], writes=[("catT", co)])


def emit_pool(nc, P, st, sb, ps, l, C, w_p, pscale, pT, catT, last):
    W = L + 16
    A = sb("pA", [128, W], F32, st)
    S = [sb("pS%d" % i, [128, W], F32, st) for i in range(4)]
    dT = sb("dT", [128, 2, NT], BF16, st)
    wst = sb("wst", [128, 2, 128], F32, st)
    wbd = sb("wbd", [128, 2, 128], BF16, st)
    psc = sb("psc", [128, 2], F32, st)
    tmpb = sb("tmpb", [128, 8], F32, st)
    ppo = [ps("ppo%d" % i, [128, 512], F32, st) for i in range(2)]
    P.op("pool", lambda e: e.memset(wst[:], 0.0), writes=["wst"])
    for gi in range(4):
        c, hf = gi // 2, gi % 2
        P.dma("sp", lambda e, gi=gi, c=c, hf=hf: e.dma_start(out=wst[hf * 64:(hf + 1) * 64, c, hf * 64:(hf + 1) * 64], in_=w_p[l, gi]),
              reads=["wst"], writes=["wst"])
    P.op("dve", lambda e: e.tensor_copy(out=wbd[:], in_=wst[:]), reads=["wst"], writes=["wbd"])
    P.dma("sp", lambda e: e.dma_start(out=psc[:], in_=pscale[l]), writes=["psc"])
    shifts = [(-1, 0), (-1, 1), (-2, 2), (-4, 4)]
    ext = [7, 6, 4, 0]
    segs = [(0, L)] if last else [(0, L), (L, LC)]
    for (s0, n) in segs:
        for c in range(2):
            P.op("pool", lambda e: e.memset(A[:, 0:8], 0.0), writes=["A"])
            P.op("pool", lambda e, n=n: e.memset(A[:, 8 + n:16 + n], 0.0), writes=["A"])
            P.op("act", lambda e, c=c, s0=s0, n=n: e.activation(out=A[:, 8:8 + n], in_=pT[:, c, s0:s0 + n], func=AF.Copy), reads=["pT"], writes=["A"])
            nst = 2 if c == 0 else 4
            for si in range(nst):
                src = A if si == 0 else S[si - 1]
                sname = "A" if si == 0 else ("S", si - 1)
                lo = 8 - ext[si]
                hi = 8 + n + ext[si]
                a, bsh = shifts[si]
                eng = "dve" if si % 2 == 0 else "pool"
                P.op(eng, lambda e, si=si, src=src, lo=lo, hi=hi, a=a, bsh=bsh: e.tensor_tensor(out=S[si][:, lo:hi], in0=src[:, lo + a:hi + a],
                                                                                            in1=src[:, lo + bsh:hi + bsh], op=ALU.add),
                     reads=[sname], writes=[("S", si)])
            for hf in range(2):
                si = (0, 1)[hf] if c == 0 else (2, 3)[hf]
                pr = slice(hf * 64, (hf + 1) * 64)
                P.op("dve", lambda e, si=si, pr=pr, c=c, s0=s0, n=n: e.scalar_tensor_tensor(out=dT[pr, c, s0:s0 + n], in0=S[si][pr, 8:8 + n],
                                                                                        scalar=C["invw"][pr, c:c + 1], in1=A[pr, 8:8 + n],
                                                                                        op0=ALU.mult, op1=ALU.subtract),
                     reads=[("S", si), "A"], writes=["dT"])
                for (bo, col) in ((0, 8), (8, 8 + n - 8)):
                    P.op("dve", lambda e, si=si, pr=pr, c=c, bo=bo, col=col: e.tensor_tensor(out=tmpb[pr, :], in0=S[si][pr, col:col + 8],
                                                                                         in1=C["invb"][pr, c, bo:bo + 8], op=ALU.mult),
                         reads=[("S", si)], writes=["tmpb"])
                    P.op("dve", lambda e, pr=pr, c=c, s0=s0, col=col: e.tensor_tensor(out=dT[pr, c, s0 + col - 8:s0 + col], in0=tmpb[pr, :],
                                                                                  in1=A[pr, col:col + 8], op=ALU.subtract),
                         reads=["tmpb", "A"], writes=["dT"])
    it = 0
    for c in range(2):
        for (t0, n) in (TB[:4] if last else TB):
            b = it % 2
            it += 1
            P.op("pe", lambda e, b=b, c=c, t0=t0, n=n: e.matmul(ppo[b][:, :n], lhsT=wbd[:, c, :], rhs=dT[:, c, t0:t0 + n], start=True, stop=True),
                 reads=["wbd", "dT"], writes=[("ppo", b)])
            P.op("act", lambda e, b=b, c=c, t0=t0, n=n: e.activation(out=catT[:, 2 + c, t0:t0 + n], in_=ppo[b][:, :n], func=AF.Copy,
                                                                    scale=psc[:, c:c + 1]),
                 reads=[("ppo", b), "psc"], writes=[("catT", 2 + c)])


def emit_attn(nc, P, st, sb, ps, l, C, qT, kT, v, catT, last):
    NB = 4
    pe_ = [sb("pexp%d" % i, [128, 512], BF16, st) for i in range(NB)]
    rden = [sb("rden%d" % i, [128, 512], F32, st) for i in range(2)]
    psc = [ps("psc%d" % i, [128, 512], F32, st) for i in range(NB)]
    po = [ps("pao%d" % i, [128, 512], F32, st) for i in range(2)]
    pden = [ps("pden%d" % i, [128, 512], F32, st) for i in range(2)]
    scale = 1.0 / math.sqrt(128.0)
    jobs = []
    for h in range(4):
        for qb in range(4):
            jobs.append((h, qb * 512, 512, list(range(NTILE))))
        if not last:
            jobs.append((h, L, LC, [16, 17]))
    items = []
    for ji, (h, q0, n, ktiles) in enumerate(jobs):
        for i, kt in enumerate(ktiles):
            items.append((ji, h, q0, n, kt, i, len(ktiles)))

    def s0(ii):
        ji, h, q0, n, kt, i, nk = items[ii]
        b = ii % NB
        kvh = h // 2
        P.op("pe", lambda e, b=b, kvh=kvh, kt=kt, h=h, q0=q0, n=n: e.matmul(psc[b][:, :n], lhsT=kT[:, kvh, kt * 128:(kt + 1) * 128],
                                                                           rhs=qT[:, h, q0:q0 + n], start=True, stop=True),
             reads=["qT", "kT"], writes=[("psc", b)])
        P.op("act", lambda e, b=b, n=n: e.activation(out=pe_[b][:, :n], in_=psc[b][:, :n], func=AF.Exp, scale=scale, bias=-SM_SHIFT),
             reads=[("psc", b)], writes=[("pexp", b)])

    def s1(ii):
        ji, h, q0, n, kt, i, nk = items[ii]
        b = ii % NB
        ob = ji % 2
        kvh = h // 2
        P.op("pe", lambda e, b=b, ob=ob, kt=kt, kvh=kvh, n=n, i=i, nk=nk: e.matmul(po[ob][:, :n], lhsT=v[:, kt, kvh * 128:(kvh + 1) * 128],
                                                                               rhs=pe_[b][:, :n], start=(i == 0), stop=(i == nk - 1)),
             reads=["v", ("pexp", b)], writes=[("po", ob)])
        P.op("pe", lambda e, b=b, ob=ob, n=n, i=i, nk=nk: e.matmul(pden[ob][:, :n], lhsT=C["ones_bf"][:], rhs=pe_[b][:, :n],
                                                                  start=(i == 0), stop=(i == nk - 1)),
             reads=[("pexp", b)], writes=[("pden", ob)])
        if i == nk - 1:
            P.op("dve", lambda e, ob=ob, n=n: e.reciprocal(out=rden[ob][:, :n], in_=pden[ob][:, :n]), reads=[("pden", ob)], writes=[("rden", ob)])
            P.op("dve", lambda e, ob=ob, n=n, h=h, q0=q0: e.tensor_tensor(out=catT[:, 4 + h, q0:q0 + n], in0=po[ob][:, :n], in1=rden[ob][:, :n], op=ALU.mult),
                 reads=[("po", ob), ("rden", ob)], writes=[("catT", 4 + h)])

    run_pipeline(len(items), [s0, s1], [0, 2])


def emit_outproj(nc, P, st, sb, ps, l, C, X, MOD, n2g, w_out, catT, h2T, H2, last):
    wo = sb("wo", [128, 8, D], BF16, st)
    wog = [sb("wog%d" % r, [128, 8, D], BF16, st) for r in range(2)]
    g1 = [sb("g1_%d" % r, [128, D], F32, st) for r in range(2)]
    xin = [sb("xin%d" % i, [128, D], F32, st) for i in range(3)]
    pso = [ps("pso%d" % i, [128, 2, 512], F32, st) for i in range(3)]
    wv = w_out[l].rearrange("(k p) n -> p k n", p=128)
    for h in range(4):
        P.dma("pool", lambda e, h=h: e.dma_start(out=wo[:, 2 * h:2 * h + 2, :], in_=wv[:, 2 * h:2 * h + 2, :]), writes=[("wo", h)])
    nr = 1 if last else 2
    for r in range(nr):
        P.dma("sp", lambda e, r=r: e.dma_start(out=g1[r][:], in_=bcast_rows(MOD[l, r, 2 * D:3 * D])), writes=[("g1", r)])
    for k in range(8):
        for r in range(nr):
            eng = "dve" if r == 0 else "pool"
            P.op(eng, lambda e, k=k, r=r: e.tensor_tensor(out=wog[r][:, k, :], in0=wo[:, k, :], in1=g1[r][:], op=ALU.mult),
                 reads=[("wo", k // 2), ("g1", r)], writes=[("wog", r, k)])
    ntl = 16 if last else NTILE

    def load_fn(t, b, xt_b):
        r = 0 if t < 16 else 1
        P.dma("sp", lambda e, b=b, t=t: e.dma_start(out=xin[b][:], in_=X[t * 128:(t + 1) * 128, :]), reads=[("X", t)], writes=[("xin", b)])
        for half in range(2):
            for j in range(8):
                P.op("pe", lambda e, b=b, half=half, j=j, t=t, r=r: e.matmul(pso[b][:, half, :], lhsT=catT[:, j, t * 128:(t + 1) * 128],
                                                                            rhs=wog[r][:, j, half * 512:(half + 1) * 512], start=(j == 0), stop=(j == 7)),
                     reads=[("wog", r, j), ("bigA", t)], writes=[("pso", b, half)])
        P.op("dve", lambda e, b=b, xt_b=xt_b: e.tensor_tensor(out=xt_b[:].rearrange("p (a n) -> p a n", a=2), in0=pso[b][:],
                                                             in1=xin[b][:].rearrange("p (a n) -> p a n", a=2), op=ALU.add),
             reads=[("pso", b, 0), ("pso", b, 1), ("xin", b)], writes=[("xt", b)])
        P.dma("sp", lambda e, b=b, t=t, xt_b=xt_b: e.dma_start(out=X[t * 128:(t + 1) * 128, :], in_=xt_b[:]), reads=[("xt", b)], writes=[("X", t)])

    emit_modulate(nc, P, st, sb, ps, l, X, MOD, n2g, C["ident_bf"], h2T, 1, H2, ntl, load_fn=load_fn, dst_key=lambda t: ("bigA", t))


def emit_router_logits(nc, P, st, sb, ps, l, C, w_r, h2T, AFF, last):
    ntl = 16 if last else NTILE
    wr = sb("wr", [128, 8, NE], BF16, st)
    mx = sb("rmx", [128, NTILE], F32, st)
    se = sb("rse", [128, NTILE], F32, st)
    ex = sb("rex", [128, NTILE, NE], F32, st)
    plog = ps("plog", [128, NTILE, NE], F32, st)
    P.dma("pool", lambda e: e.dma_start(out=wr[:], in_=w_r[l].rearrange("(k p) n -> p k n", p=128)), writes=["wr"])
    for t in range(ntl):
        for k in range(8):
            P.op("pe", lambda e, t=t, k=k: e.matmul(plog[:, t, :], lhsT=h2T[:, k, t * 128:(t + 1) * 128], rhs=wr[:, k, :],
                                                  start=(k == 0), stop=(k == 7)),
                 reads=["wr", "h2T"], writes=["plog"])
    P.op("dve", lambda e: e.tensor_reduce(out=mx[:, :ntl], in_=plog[:, :ntl, :], axis=AX.X, op=ALU.max), reads=["plog"], writes=["mx"])
    P.op("dve", lambda e: e.tensor_scalar(out=mx[:, :ntl], in0=mx[:, :ntl], scalar1=-1.0, scalar2=None, op0=ALU.mult), reads=["mx"], writes=["mx"])
    for t in range(ntl):
        P.op("act", lambda e, t=t: e.activation(out=ex[:, t, :], in_=plog[:, t, :], func=AF.Exp, bias=mx[:, t:t + 1], scale=1.0,
                                                accum_out=se[:, t:t + 1]),
             reads=["plog", "mx"], writes=[("ex", t), ("se", t)])
    P.op("dve", lambda e: e.reciprocal(out=se[:, :ntl], in_=se[:, :ntl]), reads=[("se", t) for t in range(ntl)], writes=["rse"])
    for t in range(ntl):
        P.op("dve", lambda e, t=t: e.tensor_scalar(out=AFF[:, t, :], in0=ex[:, t, :], scalar1=se[:, t:t + 1], scalar2=None, op0=ALU.mult),
             reads=[("ex", t), "rse"], writes=[("aff", t)])


def emit_topk(nc, P, st, sb, ps, l, C, AFF, IDXP, GP, last):
    ntl = 16 if last else NTILE
    affT = sb("affT", [NE, NT], F32, st)
    work = sb("rwork", [NE, L], F32, st)
    workc = sb("rworkc", [NE, LC], F32, st)
    vals = sb("rvals", [NE, CAPT], F32, st)
    idxu = sb("ridxu", [NE, CAPT], U32, st)
    idxf = sb("ridxf", [NE, CAPT], F32, st)
    prt = ps("prt", [128, 512], F32, st)
    for bk in range((ntl + 3) // 4):
        nn = min(4, ntl - bk * 4)
        for q in range(nn):
            t = bk * 4 + q
            P.op("pe", lambda e, t=t, q=q: e.transpose(out=prt[:NE, q * 128:(q + 1) * 128], in_=AFF[:, t, :], identity=C["ident_f"][:]),
                 reads=[("aff", t)], writes=["prt"])
        P.op("act", lambda e, bk=bk, nn=nn: e.activation(out=affT[:, bk * 512:bk * 512 + nn * 128], in_=prt[:NE, :nn * 128], func=AF.Copy),
             reads=["prt"], writes=["affT"])
    P.op("dve", lambda e: e.tensor_copy(out=work[:], in_=affT[:, :L]), reads=["affT"], writes=["work"])
    for r in range(CAPX // 8):
        sl = slice(r * 8, (r + 1) * 8)
        P.op("dve", lambda e, sl=sl: e.max(out=vals[:, sl], in_=work[:]), reads=["work"], writes=["vals"])
        P.op("dve", lambda e, sl=sl: e.max_index(out=idxu[:, sl], in_max=vals[:, sl], in_values=work[:]), reads=["work", "vals"], writes=["idxu"])
        P.op("dve", lambda e, sl=sl: e.match_replace(out=work[:], in_to_replace=vals[:, sl], in_values=work[:], imm_value=-1.0),
             reads=["work", "vals"], writes=["work"])
    P.op("dve", lambda e: e.tensor_copy(out=idxf[:, :CAPX], in_=idxu[:, :CAPX]), reads=["idxu"], writes=["idxf"])
    if not last:
        P.op("dve", lambda e: e.tensor_copy(out=workc[:], in_=affT[:, L:NT]), reads=["affT"], writes=["workc"])
        for r in range(CAPC // 8):
            sl = slice(CAPX + r * 8, CAPX + (r + 1) * 8)
            P.op("dve", lambda e, sl=sl: e.max(out=vals[:, sl], in_=workc[:]), reads=["workc"], writes=["vals"])
            P.op("dve", lambda e, sl=sl: e.max_index(out=idxu[:, sl], in_max=vals[:, sl], in_values=workc[:]), reads=["workc", "vals"], writes=["idxu"])
            P.op("dve", lambda e, sl=sl: e.match_replace(out=workc[:], in_to_replace=vals[:, sl], in_values=workc[:], imm_value=-1.0),
                 reads=["workc", "vals"], writes=["workc"])
        P.op("dve", lambda e: e.tensor_copy(out=idxf[:, CAPX:], in_=idxu[:, CAPX:]), reads=["idxu"], writes=["idxf"])
        P.op("dve", lambda e: e.tensor_scalar(out=idxf[:, CAPX:], in0=idxf[:, CAPX:], scalar1=float(L), scalar2=None, op0=ALU.add),
             reads=["idxf"], writes=["idxf"])
    rows = [128, 128, 32]
    for j in range(2 if last else 3):
        c0 = j * 128
        for q, src, dstt, nm in ((0, idxf, IDXP, "IDXP"), (1, vals, GP, "GP")):
            P.op("pe", lambda e, j=j, c0=c0, q=q, src=src: e.transpose(out=prt[:rows[j], q * NE:(q + 1) * NE], in_=src[:, c0:c0 + rows[j]],
                                                                      identity=C["ident_f"][:NE, :NE]),
                 reads=["idxf", "vals", "affT"], writes=["prt"])
            P.op("dve", lambda e, j=j, q=q, dstt=dstt: e.tensor_copy(out=dstt[:rows[j], :, j], in_=prt[:rows[j], q * NE:(q + 1) * NE]),
                 reads=["prt"], writes=[nm])


def emit_experts(nc, P, st, sb, ps, l, C, MOD, w_g, w_u, w_d, H2, X, IDXP, GP, last, AFF):
    g2 = [sb("g2_%d" % r, [128, D], F32, st) for r in range(2)]
    W = [[sb("W%d_%d" % (i, j), [128, 8, D], BF16, st) for j in range(3)] for i in range(2)]
    xs = [sb("xs%d" % i, [128, 3, D], BF16, st) for i in range(2)]
    xsT = [sb("xsT%d" % i, [128, 8, CAPT], BF16, st) for i in range(2)]
    sa = [sb("sa%d" % i, [128, CAPT], F32, st) for i in range(2)]
    hh = sb("hh", [128, 8, CAPT], BF16, st)
    yo = [[sb("yo%d_%d" % (q, i), [128, D], F32, st) for i in range(3)] for q in range(2)]
    ptx = ps("ptx", [128, 8, 128], BF16, st)
    sc_ev = {}
    pa = [ps("pa%d" % i, [128, 512], F32, st) for i in range(2)]
    pu = [ps("pu%d" % i, [128, 512], F32, st) for i in range(2)]
    py = [ps("py%d" % i, [128, 512], F32, st) for i in range(2)]
    nj = 2 if last else 3
    ncol = CAPX if last else CAPT
    rows = [128, 128, 32]
    srcs = (w_g, w_u, w_d)
    cnt = {"fi": 0, "pi": 0}

    def load_w(ex):
        wb = ex % 2
        for i in range(3):
            wv = srcs[i][l, ex].rearrange("(k p) n -> p k n", p=128)
            for h in range(2):
                P.dma("pool", lambda e, wb=wb, i=i, h=h, wv=wv: e.dma_start(out=W[wb][i][:, 4 * h:4 * h + 4, :], in_=wv[:, 4 * h:4 * h + 4, :]),
                      writes=[("W", wb, i, h)])

    def gather(ex):
        wb = ex % 2
        for j in range(nj):
            P.dma("pool", lambda e, wb=wb, j=j, ex=ex: e.indirect_dma_start(out=xs[wb][:rows[j], j, :], out_offset=None,
                                                                            in_=H2[0:(L if last else NT), :],
                                                                            in_offset=IOA(ap=IDXP[:rows[j], ex, j:j + 1], axis=0)),
                  reads=["H2", "IDXP"], writes=[("xs", wb, j)])

    def transposes(ex):
        wb = ex % 2
        for j in range(nj):
            for k in range(8):
                P.op("pe", lambda e, wb=wb, j=j, k=k: e.transpose(out=ptx[:, k, :rows[j]], in_=xs[wb][:rows[j], j, k * 128:(k + 1) * 128],
                                                                 identity=C["ident_bf"][:rows[j], :rows[j]]),
                     reads=[("xs", wb, j)], writes=["ptx"])
            P.op("act", lambda e, wb=wb, j=j: e.activation(out=xsT[wb][:, :, j * 128:j * 128 + rows[j]], in_=ptx[:, :, :rows[j]], func=AF.Copy),
                 reads=["ptx"], writes=[("xsT", wb)])

    def gate_up(ex):
        wb = ex % 2
        for fc in range(8):
            fb = cnt["fi"] % 2
            cnt["fi"] += 1
            for k in range(8):
                P.op("pe", lambda e, wb=wb, fb=fb, fc=fc, k=k: e.matmul(pa[fb][:, :ncol], lhsT=W[wb][0][:, k, fc * 128:(fc + 1) * 128],
                                                                       rhs=xsT[wb][:, k, :ncol], start=(k == 0), stop=(k == 7)),
                     reads=[("W", wb, 0, k // 4), ("xsT", wb)], writes=[("pa", fb)])
            for k in range(8):
                P.op("pe", lambda e, wb=wb, fb=fb, fc=fc, k=k: e.matmul(pu[fb][:, :ncol], lhsT=W[wb][1][:, k, fc * 128:(fc + 1) * 128],
                                                                       rhs=xsT[wb][:, k, :ncol], start=(k == 0), stop=(k == 7)),
                     reads=[("W", wb, 1, k // 4), ("xsT", wb)], writes=[("pu", fb)])
            P.op("act", lambda e, fb=fb: e.activation(out=sa[fb][:, :ncol], in_=pa[fb][:, :ncol], func=AF.Silu), reads=[("pa", fb)], writes=[("sa", fb)])
            P.op("dve", lambda e, fb=fb, fc=fc: e.tensor_tensor(out=hh[:, fc, :ncol], in0=sa[fb][:, :ncol], in1=pu[fb][:, :ncol], op=ALU.mult),
                 reads=[("sa", fb), ("pu", fb)], writes=[("hh", fc)])

    def down(ex):
        wb = ex % 2
        for j in range(nj):
            r = 0 if j < 2 else 1
            rw = rows[j]
            for half in range(2):
                pb = cnt["pi"] % 2
                cnt["pi"] += 1
                for fc in range(8):
                    P.op("pe", lambda e, wb=wb, pb=pb, fc=fc, j=j, rw=rw, half=half: e.matmul(py[pb][:rw, :], lhsT=hh[:, fc, j * 128:j * 128 + rw],
                                                                                           rhs=W[wb][2][:, fc, half * 512:(half + 1) * 512],
                                                                                           start=(fc == 0), stop=(fc == 7)),
                         reads=[("W", wb, 2, fc // 4), ("hh", fc)], writes=[("py", pb)])
                P.op("dve", lambda e, pb=pb, j=j, rw=rw, half=half, ex=ex, r=r, wb=wb: e.scalar_tensor_tensor(
                    out=yo[wb][j][:rw, half * 512:(half + 1) * 512], in0=py[pb][:rw, :], scalar=GP[:rw, ex, j:j + 1],
                    in1=g2[r][:rw, half * 512:(half + 1) * 512], op0=ALU.mult, op1=ALU.mult),
                    reads=[("py", pb), "GP", ("g2", r)], writes=[("yo", wb, j, half)])

    def scatter(ex):
        wb = ex % 2
        prev = sc_ev.get(ex - 1, [])
        evs = []
        for j in range(nj):
            rw = rows[j]
            evs.append(P.dma("pool", lambda e, j=j, rw=rw, ex=ex, wb=wb: e.indirect_dma_start(
                out=X, out_offset=IOA(ap=IDXP[:rw, ex, j:j + 1], axis=0), in_=yo[wb][j][:rw, :], in_offset=None, compute_op=ALU.add),
                reads=[("yo", wb, j, 0), ("yo", wb, j, 1), "IDXP"], writes=[("Xsc", ex, j)], after=prev))
        sc_ev[ex] = evs

    load_w(0)
    load_w(1)
    emit_topk(nc, P, st, sb, ps, l, C, AFF, IDXP, GP, last)
    gather(0)
    gather(1)
    for r in range(2):
        P.dma("sp", lambda e, r=r: e.dma_start(out=g2[r][:], in_=bcast_rows(MOD[l, r, 5 * D:6 * D])), writes=[("g2", r)])
    transposes(0)
    for ex in range(NE):
        gate_up(ex)
        if ex + 1 < NE:
            transposes(ex + 1)
        down(ex)
        scatter(ex)
        if ex + 2 < NE:
            gather(ex + 2)
            load_w(ex + 2)


def kernel(**inputs):
    depth = inputs["ada_w"].shape[0]
    nb = inputs["x"].shape[0]
    nc, _ = build_program(depth)
    f32 = lambda a: np.ascontiguousarray(np.asarray(a, dtype=np.float32))
    shared = {}
    for nm in ["ada_w", "ada_b", "norm1_g", "norm2_g", "w_in", "w_fourier", "w_pool", "q_norm_g", "k_norm_g",
               "w_out", "w_router", "w_gate", "w_up", "w_down"]:
        shared[nm] = f32(inputs[nm])
    shared["pool_scale"] = np.ascontiguousarray(f32(inputs["pool_scale"]).reshape(depth, 2, 128).transpose(0, 2, 1))
    for k, v in make_consts().items():
        shared["k_" + k] = v
    x = f32(inputs["x"])
    ctx = f32(inputs["ctx"])
    c = f32(inputs["c"])
    c_ctx = f32(inputs["c_ctx"])
    in_maps = []
    for b in range(nb):
        m = dict(shared)
        m["x"] = np.ascontiguousarray(x[b])
        m["ctx"] = np.ascontiguousarray(ctx[b])
        cT = np.stack([c[b], c_ctx], axis=-1)
        m["cT"] = np.ascontiguousarray(cT.reshape(8, 128, 2).transpose(1, 0, 2))
        in_maps.append(m)
    res = run_bass_kernel_spmd(nc, in_maps, core_ids=list(range(nb)))
    return np.stack([np.asarray(r["out"], dtype=np.float32) for r in res.results], axis=0)
```

```python
import math
from contextlib import ExitStack

import numpy as np
import ml_dtypes
import concourse.bass as bass
import concourse.mybir as mybir
from concourse.bass_utils import run_bass_kernel_spmd

F32 = mybir.dt.float32
BF16 = mybir.dt.bfloat16
U32 = mybir.dt.uint32
I32 = mybir.dt.int32
ALU = mybir.AluOpType
AF = mybir.ActivationFunctionType
AX = mybir.AxisListType
IOA = bass.IndirectOffsetOnAxis

D = 1024
L = 2048
LC = 256
NT = L + LC
NTILE = NT // 128
NE = 16
CAPX = 256
CAPC = 32
CAPT = CAPX + CAPC
EPS = 1e-6
SM_SHIFT = 12.0
NDMA_SEM = 6


class Prog:
    sems = None
    gcnt = None

    @classmethod
    def begin_program(cls, nc, stack):
        cls.sems = {}
        cls.gcnt = {}
        keys = ["pe", "act", "dve", "pool"] + ["dma_%s_%d" % (q, i) for q in ("sp", "act", "pool") for i in range(NDMA_SEM)]
        for k in keys:
            cls.sems[k] = stack.enter_context(nc.semaphore("s_" + k))
            cls.gcnt[k] = 0

    def __init__(self, nc, same_engine_sync=True):
        self.nc = nc
        self.same = same_engine_sync
        self.ops = {e: [] for e in ("pe", "act", "dve", "pool", "sp")}
        self.cidx = {e: 0 for e in self.ops}
        self.dcnt = Prog.gcnt
        self.known = {e: {} for e in self.ops}
        self.lastw = {}
        self.readers = {}
        self.dma_rr = {e: 0 for e in self.ops}
        self.miles = {e: set() for e in self.ops}

    def _deps(self, reads, writes):
        deps = []
        for r in reads:
            ev = self.lastw.get(r)
            if ev is not None:
                deps.append(ev)
        for w in writes:
            ev = self.lastw.get(w)
            if ev is not None:
                deps.append(ev)
            deps.extend(self.readers.get(w, ()))
        return deps

    def _commit(self, ev, reads, writes):
        for w in writes:
            self.lastw[w] = ev
            self.readers[w] = []
        for r in reads:
            if r in writes:
                continue
            self.readers.setdefault(r, []).append(ev)

    def _waits(self, eng, deps):
        need = {}
        for (kind, sk, val) in deps:
            if kind == "c" and sk == eng and (eng == "pe" or not self.same):
                continue
            key = (kind, sk)
            if self.known[eng].get(key, 0) >= val:
                continue
            if need.get(key, 0) < val:
                need[key] = val
        out = []
        for key, val in need.items():
            self.known[eng][key] = val
            if key[0] == "c":
                self.miles[key[1]].add(val)
            out.append((key[0], key[1], val))
        return out

    def op(self, eng, fn, reads=(), writes=()):
        reads = tuple(reads)
        writes = tuple(writes)
        waits = self._waits(eng, self._deps(reads, writes))
        self.cidx[eng] += 1
        ev = ("c", eng, self.cidx[eng])
        self.ops[eng].append((waits, fn, ev))
        self._commit(ev, reads, writes)
        return ev

    def dma(self, eng, fn, reads=(), writes=(), after=()):
        reads = tuple(reads)
        writes = tuple(writes)
        i = self.dma_rr[eng]
        self.dma_rr[eng] = (i + 1) % NDMA_SEM
        sk = "dma_%s_%d" % (eng, i)
        deps = self._deps(reads, writes) + [("d", sk, self.dcnt[sk])] + list(after)
        waits = self._waits(eng, deps)
        self.dcnt[sk] += 16
        ev = ("d", sk, self.dcnt[sk])
        self.ops[eng].append((waits, fn, ev))
        self._commit(ev, reads, writes)
        return ev

    def emit(self):
        nc = self.nc
        sems = Prog.sems
        final = [("c", e, self.cidx[e]) for e in ("pe", "act", "dve", "pool") if self.cidx[e] > 0]
        final += [("d", k, v) for k, v in self.dcnt.items() if k.startswith("dma_") and v > 0]
        waits = self._waits("sp", final)
        self.ops["sp"].append((waits, None, None))
        cmap = {}
        for e in ("pe", "act", "dve", "pool"):
            for rank, idx in enumerate(sorted(self.miles[e])):
                cmap[(e, idx)] = Prog.gcnt[e] + rank + 1
            Prog.gcnt[e] += len(self.miles[e])
        with nc.Block() as block:
            engmap = {"pe": block.tensor, "act": block.scalar, "dve": block.vector,
                      "pool": block.gpsimd, "sp": block.sync}
            for e, lst in self.ops.items():
                if not lst:
                    continue

                def body(engine, lst=lst):
                    for w, fn, ev in lst:
                        for (kind, sk, val) in w:
                            if kind == "c":
                                engine.wait_ge(sems[sk], cmap[(sk, val)])
                            else:
                                engine.wait_ge(sems[sk], val)
                        if fn is None:
                            continue
                        ins = fn(engine)
                        if ev[0] == "d":
                            ins.then_inc(sems[ev[1]], 16)
                        elif (ev[1], ev[2]) in cmap:
                            ins.then_inc(sems[ev[1]], 1)
                engmap[e](body)


def _bf(a):
    return np.ascontiguousarray(a.astype(np.float32)).astype(ml_dtypes.bfloat16)


def make_consts():
    c = {}
    c["ident_bf"] = _bf(np.eye(128))
    c["ident_f"] = np.eye(128, dtype=np.float32)
    c["onesm_bf"] = _bf(np.full((128, 128), 1.0 / 128.0))
    c["ones_bf"] = _bf(np.ones((128, 128)))
    d = np.arange(128)
    partner = np.where(d % 64 < 32, d + 32, d - 32)
    perm = np.zeros((128, 128), np.float32)
    perm[partner, d] = 1.0
    c["perm_f"] = perm
    t = np.arange(L)
    inv = 10000.0 ** (-(np.arange(0, 64, 2, dtype=np.float64)) / 64.0)
    pos = np.where((d // 64)[:, None] == 0, (t // 64)[None, :], (t % 64)[None, :]).astype(np.float64)
    ang = (pos.astype(np.float32) * inv.astype(np.float32)[d % 32][:, None]).astype(np.float64)
    sgn = np.where(d % 64 < 32, -1.0, 1.0)[:, None]
    c["cosT"] = np.cos(ang).astype(np.float32)
    c["sinT"] = (sgn * np.sin(ang)).astype(np.float32)
    i = np.arange(256)
    same = (i[:, None] // 64) == (i[None, :] // 64)
    ph = 2 * np.pi * ((i[:, None] % 64) * (i[None, :] % 64) % 64) / 64.0
    bdc = np.where(same, np.cos(ph), 0.0) / 8.0
    bds = np.where(same, -np.sin(ph), 0.0) / 8.0
    bd = np.concatenate([bdc, bds], axis=1)
    c["bd"] = _bf(bd.reshape(2, 128, 512).transpose(1, 0, 2))
    for n, nm in ((L, "x"), (LC, "c")):
        k = np.arange(n)
        ph = 2 * np.pi * ((k[:, None] * k[None, :]) % n) / n
        c["cn_" + nm] = _bf(np.cos(ph) / math.sqrt(n))
        c["sn_" + nm] = _bf(np.sin(ph) / math.sqrt(n))
    p = np.arange(128)
    wins = np.array([[2, 4], [8, 16]])
    invw = np.zeros((128, 2), np.float32)
    invb = np.zeros((128, 2, 16), np.float32)
    for ch in range(2):
        for pp in range(128):
            w = wins[ch, pp // 64]
            invw[pp, ch] = 1.0 / w
            for ii in range(8):
                tt = ii
                cnt = min(tt + w // 2, 1 << 30) - max(tt - w // 2, 0)
                invb[pp, ch, ii] = 1.0 / cnt
                r = 7 - ii
                cnt = (w // 2 if r >= w // 2 else r + 1) + w // 2
                cnt = min(w // 2, r + 1) + w // 2
                invb[pp, ch, 8 + ii] = 1.0 / cnt
    c["invw"] = invw
    c["invb"] = invb
    return c


CONST_SPECS = [
    ("ident_bf", [128, 128], BF16), ("ident_f", [128, 128], F32), ("onesm_bf", [128, 128], BF16),
    ("ones_bf", [128, 128], BF16), ("perm_f", [128, 128], F32), ("cosT", [128, L], F32),
    ("sinT", [128, L], F32), ("bd", [128, 2, 512], BF16), ("cn_x", [L, L], BF16), ("sn_x", [L, L], BF16),
    ("cn_c", [LC, LC], BF16), ("sn_c", [LC, LC], BF16), ("invw", [128, 2], F32), ("invb", [128, 2, 16], F32),
]


def build_program(depth, dbg=()):
    nc = bass.Bass("TRN2", target_bir_lowering=False)
    dt_in = {}

    def inp(name, shape, dt=F32):
        dt_in[name] = nc.dram_tensor(name, list(shape), dt, kind="ExternalInput").ap()
        return dt_in[name]

    x_in = inp("x", [L, D])
    ctx_in = inp("ctx", [LC, D])
    cT_in = inp("cT", [128, 8, 2])
    ada_w = inp("ada_w", [depth, D, 6 * D])
    ada_b = inp("ada_b", [depth, 6 * D])
    n1g = inp("norm1_g", [depth, D])
    n2g = inp("norm2_g", [depth, D])
    w_in = inp("w_in", [depth, D, 1536])
    w_f = inp("w_fourier", [depth, 256, 256])
    w_p = inp("w_pool", [depth, 4, 64, 64])
    pscale = inp("pool_scale", [depth, 128, 2])
    qg = inp("q_norm_g", [depth, 128])
    kg = inp("k_norm_g", [depth, 128])
    w_out = inp("w_out", [depth, D, D])
    w_r = inp("w_router", [depth, D, NE])
    with_experts = not any(d.startswith("stop_") for d in dbg)
    w_g = inp("w_gate", [depth, NE, D, D]) if with_experts else None
    w_u = inp("w_up", [depth, NE, D, D]) if with_experts else None
    w_d = inp("w_down", [depth, NE, D, D]) if with_experts else None
    cst = {nm: inp("k_" + nm, shp, dt) for nm, shp, dt in CONST_SPECS}
    out = nc.dram_tensor("out", [L, D], F32, kind="ExternalOutput").ap()
    dbg_out = {}

    def dram(name, shape, dt):
        if name in dbg:
            dbg_out[name] = nc.dram_tensor(name, list(shape), dt, kind="ExternalOutput").ap()
            return dbg_out[name]
        return nc.dram_tensor(name, list(shape), dt, kind="Internal").ap()

    X = dram("X", [NT, D], F32)
    MOD = dram("MOD", [depth, 2, 6 * D], F32)
    H2 = dram("H2", [NT, D], BF16)

    top = ExitStack()
    with top:
        uid = [0]

        def sb(name, shape, dt, st=top):
            uid[0] += 1
            return st.enter_context(nc.sbuf_tensor("%s_%d" % (name, uid[0]), list(shape), dt))

        def ps(name, shape, dt=F32, st=top):
            uid[0] += 1
            return st.enter_context(nc.psum_tensor("%s_%d" % (name, uid[0]), list(shape), dt))

        Prog.begin_program(nc, top)
        ident_bf = sb("ident_bf", [128, 128], BF16)
        ident_f = sb("ident_f", [128, 128], F32)
        onesm_bf = sb("onesm_bf", [128, 128], BF16)
        ones_bf = sb("ones_bf", [128, 128], BF16)
        perm_f = sb("perm_f", [128, 128], F32)
        cosT = sb("cosT", [128, L], F32)
        sinT = sb("sinT", [128, L], F32)
        bd = sb("bd", [128, 2, 512], BF16)
        cn_c = sb("cn_c", [128, 2, LC], BF16)
        sn_c = sb("sn_c", [128, 2, LC], BF16)
        invw = sb("invw", [128, 2], F32)
        invb = sb("invb", [128, 2, 16], F32)
        scT = sb("scT", [128, 8, 2], BF16)
        IDXP = sb("IDXP", [128, NE, 3], U32)
        AFF = sb("AFF", [128, NTILE, NE], F32)
        GP = sb("GP", [128, NE, 3], F32)

        with ExitStack() as st:
            P = Prog(nc)
            cTf = sb("cTf", [128, 8, 2], F32, st)
            for nm, t in (("ident_bf", ident_bf), ("ident_f", ident_f), ("onesm_bf", onesm_bf), ("ones_bf", ones_bf),
                          ("perm_f", perm_f), ("cosT", cosT), ("sinT", sinT), ("bd", bd), ("invw", invw), ("invb", invb)):
                P.dma("sp", lambda e, t=t, nm=nm: e.dma_start(out=t[:], in_=cst[nm]), writes=[nm])
            P.dma("sp", lambda e: e.dma_start(out=cn_c[:], in_=cst["cn_c"].rearrange("(t p) k -> p t k", p=128)), writes=["cn_c"])
            P.dma("sp", lambda e: e.dma_start(out=sn_c[:], in_=cst["sn_c"].rearrange("(t p) k -> p t k", p=128)), writes=["sn_c"])
            if depth == 1 or "notlast" in dbg:
                P.dma("sp", lambda e: e.dma_start(out=X[L:NT, :], in_=ctx_in), writes=["X"])
            P.op("pool", lambda e: e.memset(IDXP[:], 0), writes=["IDXP"])
            P.op("pool", lambda e: e.memset(GP[:], 0.0), writes=["GP"])
            P.dma("sp", lambda e: e.dma_start(out=cTf[:], in_=cT_in), writes=["cTf"])
            P.op("act", lambda e: e.activation(out=scT[:], in_=cTf[:], func=AF.Silu), reads=["cTf"], writes=["scT"])
            emit_ada(nc, P, st, sb, ps, 0, scT, ada_w, ada_b, MOD)
            P.emit()

        if "stop_init" in dbg:
            depth_run = 0
        else:
            depth_run = depth

        C = {"ident_bf": ident_bf, "ident_f": ident_f, "onesm_bf": onesm_bf, "ones_bf": ones_bf, "perm_f": perm_f,
             "cosT": cosT, "sinT": sinT, "bd": bd, "cn_c": cn_c, "sn_c": sn_c, "invw": invw, "invb": invb}
        def xsrc0(t):
            return x_in[t * 128:(t + 1) * 128, :] if t < 16 else ctx_in[(t - 16) * 128:(t - 15) * 128, :]

        stop = [d for d in dbg if d.startswith("stop_")]
        stop = stop[0] if stop else None
        for l in range(depth_run):
            last = (l == depth - 1) and ("notlast" not in dbg)
            done = False
            with ExitStack() as ast:
                bigA = sb("bigA", [128, 8, NT], BF16, ast)
                with ExitStack() as st:
                    P = Prog(nc)
                    emit_modulate(nc, P, st, sb, ps, l, X, MOD, n1g, ident_bf, bigA, 0, None, NTILE, xsrc=xsrc0 if l == 0 else None)
                    P.emit()
                if stop == "stop_1a":
                    done = True
                if not done:
                    with ExitStack() as pst:
                        fT = sb("fT", [128, 2, NT], BF16, pst)
                        pT = sb("pT", [128, 2, NT], F32, pst)
                        qT = sb("qT", [128, 4, NT], BF16, pst)
                        kT = sb("kT", [128, 2, NT], BF16, pst)
                        vv = sb("vv", [128, NTILE, 256], BF16, pst)
                        with ExitStack() as st:
                            P = Prog(nc)
                            emit_proj(nc, P, st, sb, ps, l, C, w_in, qg, kg, bigA, fT, pT, qT, kT, vv, last)
                            P.emit()
                        if stop == "stop_1b":
                            with ExitStack() as st:
                                P = Prog(nc)
                                for nm, t, shp, dt in (("fT", fT, [128, 2, NT], BF16), ("pT", pT, [128, 2, NT], F32), ("qT", qT, [128, 4, NT], BF16),
                                                       ("kT", kT, [128, 2, NT], BF16), ("vv", vv, [128, NTILE, 256], BF16)):
                                    dd = nc.dram_tensor("d_" + nm, shp, dt, kind="ExternalOutput").ap()
                                    dbg_out["d_" + nm] = dd
                                    P.dma("sp", lambda e, dd=dd, t=t: e.dma_start(out=dd, in_=t[:]), writes=["d_" + nm])
                                P.emit()
                            done = True
                        else:
                            with ExitStack() as st:
                                P = Prog(nc)
                                emit_fourier(nc, P, st, sb, ps, l, C, cst, w_f, fT, bigA, last)
                                P.emit()
                            with ExitStack() as st:
                                P = Prog(nc)
                                if l + 1 < depth:
                                    emit_ada(nc, P, st, sb, ps, l + 1, scT, ada_w, ada_b, MOD)
                                emit_pool(nc, P, st, sb, ps, l, C, w_p, pscale, pT, bigA, last)
                                P.emit()
                            with ExitStack() as st:
                                P = Prog(nc)
                                emit_attn(nc, P, st, sb, ps, l, C, qT, kT, vv, bigA, last)
                                P.emit()
                if stop == "stop_1c":
                    done = True
                if not done:
                    with ExitStack() as st:
                        P = Prog(nc)
                        emit_outproj(nc, P, st, sb, ps, l, C, X, MOD, n2g, w_out, bigA, bigA, H2, last, xsrc=xsrc0 if l == 0 else None)
                        P.emit()
                    if stop == "stop_1d":
                        done = True
                if not done:
                    with ExitStack() as st:
                        P = Prog(nc)
                        emit_router_logits(nc, P, st, sb, ps, l, C, w_r, bigA, AFF, last)
                        P.emit()
                    if stop == "stop_1e":
                        done = True
                if done and ("hT" in dbg):
                    with ExitStack() as st:
                        P = Prog(nc)
                        hT_d = nc.dram_tensor("hT", [128, 8, NT], BF16, kind="ExternalOutput").ap()
                        dbg_out["hT"] = hT_d
                        P.dma("sp", lambda e: e.dma_start(out=hT_d, in_=bigA[:]), writes=["hT_d"])
                        P.emit()
            if done:
                break
            with ExitStack() as st:
                P = Prog(nc)
                emit_experts(nc, P, st, sb, ps, l, C, MOD, w_g, w_u, w_d, H2, X, IDXP, GP, last, AFF)
                P.emit()

        with ExitStack() as st:
            P = Prog(nc)
            P.dma("sp", lambda e: e.dma_start(out=out, in_=X[0:L, :]), writes=["out"])
            if "IDXP" in dbg:
                i_d = nc.dram_tensor("d_IDXP", [128, NE, 3], U32, kind="ExternalOutput").ap()
                g_d = nc.dram_tensor("d_GP", [128, NE, 3], F32, kind="ExternalOutput").ap()
                dbg_out["d_IDXP"] = i_d
                dbg_out["d_GP"] = g_d
                P.dma("sp", lambda e: e.dma_start(out=i_d, in_=IDXP[:]), writes=["i_d"])
                P.dma("sp", lambda e: e.dma_start(out=g_d, in_=GP[:]), writes=["g_d"])
            P.emit()
    return nc, list(dbg_out.keys())


def run_pipeline(n, stages, skews):
    for step in range(n + max(skews)):
        for stg, sk in zip(stages, skews):
            i = step - sk
            if 0 <= i < n:
                stg(i)


def emit_ada(nc, P, st, sb, ps, l, scT, ada_w, ada_b, MOD):
    NB = 3
    wada = [sb("wada%d" % i, [128, 8, 512], BF16, st) for i in range(NB)]
    adab = [sb("adab%d" % i, [2, 512], F32, st) for i in range(2)]
    mrow = [sb("mrow%d" % i, [2, 512], F32, st) for i in range(2)]
    pmod = [ps("pmod%d" % i, [2, 512], F32, st) for i in range(2)]
    awv = ada_w[l].rearrange("(k p) n -> p k n", p=128)

    def load(cb):
        b = cb % NB
        P.dma("pool", lambda e, b=b, cb=cb: e.dma_start(out=wada[b][:], in_=awv[:, :, cb * 512:(cb + 1) * 512]), writes=[("wada", b)])
        for r in range(2):
            P.dma("sp", lambda e, cb=cb, r=r: e.dma_start(out=adab[cb % 2][r:r + 1, :], in_=ada_b[l:l + 1, cb * 512:(cb + 1) * 512]),
                  writes=[("adab", cb % 2, r)])

    load(0)
    load(1)
    for cb in range(12):
        b = cb % NB
        pb = cb % 2
        for k in range(8):
            P.op("pe", lambda e, b=b, pb=pb, k=k: e.matmul(pmod[pb][:], lhsT=scT[:, k, :], rhs=wada[b][:, k, :], start=(k == 0), stop=(k == 7)),
                 reads=["scT", ("wada", b)], writes=[("pmod", pb)])
        P.op("dve", lambda e, pb=pb: e.tensor_tensor(out=mrow[pb][:], in0=pmod[pb][:], in1=adab[pb][:], op=ALU.add),
             reads=[("pmod", pb), ("adab", pb, 0), ("adab", pb, 1)], writes=[("mrow", pb)])
        P.dma("sp", lambda e, pb=pb, cb=cb: e.dma_start(out=MOD[l, :, cb * 512:(cb + 1) * 512], in_=mrow[pb][:]), reads=[("mrow", pb)],
              writes=[("modout", cb)])
        if cb + 2 < 12:
            load(cb + 2)


def bcast_rows(ap_row):
    return ap_row.partition_broadcast(128)


def emit_modulate(nc, P, st, sb, ps, l, X, MOD, ng, ident_bf, dstT, which, H2, ntiles, load_fn=None, dst_key=None, pre_fn=None, xsrc=None):
    sh_off = 0 if which == 0 else 3 * D
    sc_off = sh_off + D
    NBF = 3
    gnb = sb("gnb", [128, D], F32, st)
    gm = [sb("gm%d" % r, [128, D], F32, st) for r in range(2)]
    sh = [sb("sh%d" % r, [128, D], F32, st) for r in range(2)]
    xt = [sb("xt%d" % i, [128, D], F32, st) for i in range(NBF)]
    y1 = [sb("y1_%d" % i, [128, D], F32, st) for i in range(2)]
    hb = [sb("hb%d" % i, [128, D], BF16, st) for i in range(NBF)]
    junk = sb("junk", [128, D], BF16, st)
    ss = sb("ss", [128, NTILE], F32, st)
    rstd = sb("rstd", [128, NTILE], F32, st)
    ptr = [ps("ptr%d" % i, [128, 8, 128], BF16, st) for i in range(2)]
    P.dma("sp", lambda e: e.dma_start(out=gnb[:], in_=bcast_rows(ng[l])), writes=["gnb"])
    for r in range(2):
        P.dma("sp", lambda e, r=r: e.dma_start(out=gm[r][:], in_=bcast_rows(MOD[l, r, sc_off:sc_off + D])), writes=[("gm", r)])
        P.dma("sp", lambda e, r=r: e.dma_start(out=sh[r][:], in_=bcast_rows(MOD[l, r, sh_off:sh_off + D])), writes=[("sh", r)])
        P.op("dve", lambda e, r=r: e.scalar_tensor_tensor(out=gm[r][:], in0=gm[r][:], scalar=1.0, in1=gnb[:], op0=ALU.add, op1=ALU.mult),
             reads=["gnb", ("gm", r)], writes=[("gm", r)])

    def stage_a1(t):
        b = t % NBF
        if load_fn is None:
            src = xsrc(t) if xsrc else X[t * 128:(t + 1) * 128, :]
            P.dma("sp", lambda e, b=b, src=src: e.dma_start(out=xt[b][:], in_=src), reads=["X"], writes=[("xt", b)])
        else:
            load_fn(t, b, xt[b])

    def stage_a2(t):
        b = t % NBF
        yb = t % 2
        r = 0 if t < 16 else 1
        P.op("act", lambda e, b=b, t=t: e.activation(out=junk[:], in_=xt[b][:], func=AF.Square, accum_out=ss[:, t:t + 1]),
             reads=[("xt", b)], writes=["junk", ("ss", t)])
        P.op("act", lambda e, t=t: e.activation(out=rstd[:, t:t + 1], in_=ss[:, t:t + 1], func=AF.Sqrt, scale=1.0 / D, bias=EPS),
             reads=[("ss", t)], writes=[("rstd", t)])
        P.op("dve", lambda e, t=t: e.reciprocal(out=rstd[:, t:t + 1], in_=rstd[:, t:t + 1]), reads=[("rstd", t)], writes=[("rstd", t)])
        P.op("dve", lambda e, b=b, yb=yb, t=t, r=r: e.scalar_tensor_tensor(out=y1[yb][:], in0=xt[b][:], scalar=rstd[:, t:t + 1], in1=gm[r][:],
                                                                        op0=ALU.mult, op1=ALU.mult),
             reads=[("xt", b), ("rstd", t), ("gm", r)], writes=[("y1", yb)])
        P.op("pool", lambda e, b=b, yb=yb, r=r: e.tensor_tensor(out=hb[b][:], in0=y1[yb][:], in1=sh[r][:], op=ALU.add),
             reads=[("y1", yb), ("sh", r)], writes=[("hb", b)])
        if H2 is not None:
            P.dma("sp", lambda e, b=b, t=t: e.dma_start(out=H2[t * 128:(t + 1) * 128, :], in_=hb[b][:]), reads=[("hb", b)], writes=[("H2", t)])

    def stage_b(t):
        b = t % NBF
        pb = t % 2
        for k in range(8):
            P.op("pe", lambda e, b=b, pb=pb, k=k: e.transpose(out=ptr[pb][:, k, :], in_=hb[b][:, k * 128:(k + 1) * 128], identity=ident_bf[:]),
                 reads=[("hb", b)], writes=[("ptr", pb)])
        P.op("act", lambda e, pb=pb, t=t: e.activation(out=dstT[:, :, t * 128:(t + 1) * 128], in_=ptr[pb][:], func=AF.Copy),
             reads=[("ptr", pb)], writes=[dst_key(t) if dst_key else ("dstT", t)])

    if pre_fn is None:
        run_pipeline(ntiles, [stage_a1, stage_a2, stage_b], [0, 1, 2])
    else:
        run_pipeline(ntiles, [pre_fn, stage_a1, stage_a2, stage_b], [0, 2, 3, 5])


TB = [(0, 512), (512, 512), (1024, 512), (1536, 512), (2048, 256)]


def emit_proj(nc, P, st, sb, ps, l, C, w_in, qg, kg, hT, fT, pT, qT, kT, v, last):
    wi = sb("wi", [128, 8, 1536], BF16, st)
    gq = sb("gq", [128, 1], F32, st)
    gk = sb("gk", [128, 1], F32, st)
    sq = [sb("sq%d" % i, [128, 512], BF16, st) for i in range(2)]
    rs = [sb("rs%d" % i, [128, 512], F32, st) for i in range(2)]
    qn = [sb("qn%d" % i, [128, 512], F32, st) for i in range(3)]
    t1 = [sb("t1%d" % i, [128, 512], F32, st) for i in range(2)]
    t2 = [sb("t2%d" % i, [128, 512], F32, st) for i in range(2)]
    pp = [ps("pp%d" % i, [128, 512], F32, st) for i in range(3)]
    pss = [ps("pss%d" % i, [128, 512], F32, st) for i in range(2)]
    prot = [ps("prot%d" % i, [128, 512], F32, st) for i in range(2)]
    wv = w_in[l].rearrange("(k p) n -> p k n", p=128)
    for h in range(4):
        P.dma("pool", lambda e, h=h: e.dma_start(out=wi[:, 2 * h:2 * h + 2, :], in_=wv[:, 2 * h:2 * h + 2, :]), writes=[("wi", h)])
    WI = [("wi", h) for h in range(4)]
    P.dma("sp", lambda e: e.dma_start(out=gq[:], in_=qg[l].rearrange("(p o) -> p o", o=1)), writes=["gq"])
    P.dma("sp", lambda e: e.dma_start(out=gk[:], in_=kg[l].rearrange("(p o) -> p o", o=1)), writes=["gk"])
    items = [(cc, t0, n) for cc in range(4, 10) for (t0, n) in TB] + [(cc, t0, n) for cc in range(4) for (t0, n) in TB]
    for t in range(NTILE):
        items.append(("v", t, 256))

    def s0(i):
        cc, t0, n = items[i]
        b = i % 3
        if cc == "v":
            t = t0
            for k in range(8):
                P.op("pe", lambda e, b=b, k=k, t=t: e.matmul(pp[b][:, :256], lhsT=hT[:, k, t * 128:(t + 1) * 128], rhs=wi[:, k, 1280:1536],
                                                            start=(k == 0), stop=(k == 7)),
                     reads=WI + ["hT"], writes=[("pp", b)])
            P.op("act", lambda e, b=b, t=t: e.activation(out=v[:, t, :], in_=pp[b][:, :256], func=AF.Copy), reads=[("pp", b)], writes=["v"])
            return
        for k in range(8):
            P.op("pe", lambda e, b=b, k=k, cc=cc, t0=t0, n=n: e.matmul(pp[b][:, :n], lhsT=wi[:, k, cc * 128:(cc + 1) * 128],
                                                                      rhs=hT[:, k, t0:t0 + n], start=(k == 0), stop=(k == 7)),
                 reads=WI + ["hT"], writes=[("pp", b)])
        if cc < 2:
            P.op("act", lambda e, b=b, cc=cc, t0=t0, n=n: e.activation(out=fT[:, cc, t0:t0 + n], in_=pp[b][:, :n], func=AF.Copy),
                 reads=[("pp", b)], writes=["fT"])
        elif cc < 4:
            P.op("act", lambda e, b=b, cc=cc, t0=t0, n=n: e.activation(out=pT[:, cc - 2, t0:t0 + n], in_=pp[b][:, :n], func=AF.Copy),
                 reads=[("pp", b)], writes=["pT"])
        else:
            P.op("act", lambda e, b=b, n=n, i=i: e.activation(out=sq[i % 2][:, :n], in_=pp[b][:, :n], func=AF.Square),
                 reads=[("pp", b)], writes=[("sq", i % 2)])

    def s1(i):
        cc, t0, n = items[i]
        if cc == "v" or cc < 4:
            return
        b = i % 3
        b2 = i % 2
        isq = cc < 8
        gv = gq if isq else gk
        gname = "gq" if isq else "gk"
        P.op("pe", lambda e, b2=b2, n=n: e.matmul(pss[b2][:, :n], lhsT=C["onesm_bf"][:], rhs=sq[b2][:, :n], start=True, stop=True),
             reads=[("sq", b2)], writes=[("pss", b2)])
        P.op("act", lambda e, b2=b2, n=n: e.activation(out=rs[b2][:, :n], in_=pss[b2][:, :n], func=AF.Sqrt, bias=EPS),
             reads=[("pss", b2)], writes=[("rs", b2)])
        P.op("dve", lambda e, b2=b2, n=n: e.reciprocal(out=rs[b2][:, :n], in_=rs[b2][:, :n]), reads=[("rs", b2)], writes=[("rs", b2)])
        P.op("dve", lambda e, b=b, b2=b2, n=n, gv=gv: e.scalar_tensor_tensor(out=qn[b][:, :n], in0=pp[b][:, :n], scalar=gv[:, 0:1],
                                                                          in1=rs[b2][:, :n], op0=ALU.mult, op1=ALU.mult),
             reads=[("pp", b), ("rs", b2), gname], writes=[("qn", b)])

    def s2(i):
        cc, t0, n = items[i]
        if cc == "v" or cc < 4:
            return
        b = i % 3
        b2 = i % 2
        isq = cc < 8
        dst = qT[:, cc - 4, t0:t0 + n] if isq else kT[:, cc - 8, t0:t0 + n]
        dname = "qT" if isq else "kT"
        if t0 >= L:
            P.op("pool", lambda e, b=b, n=n, dst=dst: e.tensor_copy(out=dst, in_=qn[b][:, :n]), reads=[("qn", b)], writes=[dname])
            return
        P.op("pe", lambda e, b=b, b2=b2, n=n: e.matmul(prot[b2][:, :n], lhsT=C["perm_f"][:], rhs=qn[b][:, :n], start=True, stop=True),
             reads=[("qn", b)], writes=[("prot", b2)])
        P.op("pool", lambda e, b=b, b2=b2, n=n, t0=t0: e.tensor_tensor(out=t1[b2][:, :n], in0=qn[b][:, :n], in1=C["cosT"][:, t0:t0 + n], op=ALU.mult),
             reads=[("qn", b)], writes=[("t1", b2)])
        P.op("dve", lambda e, b2=b2, n=n, t0=t0: e.tensor_tensor(out=t2[b2][:, :n], in0=prot[b2][:, :n], in1=C["sinT"][:, t0:t0 + n], op=ALU.mult),
             reads=[("prot", b2)], writes=[("t2", b2)])
        P.op("pool", lambda e, b2=b2, n=n, dst=dst: e.tensor_tensor(out=dst, in0=t1[b2][:, :n], in1=t2[b2][:, :n], op=ALU.add),
             reads=[("t1", b2), ("t2", b2)], writes=[dname])

    run_pipeline(len(items), [s0, s1, s2], [0, 1, 2])


def emit_fourier(nc, P, st, sb, ps, l, C, cst, w_f, fT, catT, last):
    g = sb("g", [128, NTILE, 512], BF16, st)
    frT = sb("frT", [128, 2, NT], BF16, st)
    wf = sb("wf", [128, 2, 256], BF16, st)
    tb = [[sb("tb%d_%d" % (i, j), [128, 8, 512], BF16, st) for j in range(2)] for i in range(2)]
    pg = [ps("pg%d" % i, [128, 512], F32, st) for i in range(2)]
    pf = [ps("pf%d" % i, [128, 512], F32, st) for i in range(2)]
    po = [ps("pfo%d" % i, [128, 512], F32, st) for i in range(2)]
    P.dma("pool", lambda e: e.dma_start(out=wf[:], in_=w_f[l].rearrange("(k p) n -> p k n", p=128)), writes=["wf"])
    ntl = 16 if last else NTILE
    for t in range(ntl):
        b = t % 2
        for c in range(2):
            P.op("pe", lambda e, b=b, c=c, t=t: e.matmul(pg[b][:], lhsT=fT[:, c, t * 128:(t + 1) * 128], rhs=C["bd"][:, c, :],
                                                        start=(c == 0), stop=(c == 1)),
                 reads=["fT"], writes=[("pg", b)])
        P.op("act", lambda e, b=b, t=t: e.activation(out=g[:, t, :], in_=pg[b][:], func=AF.Copy), reads=[("pg", b)], writes=[("g", t)])
    cnv = cst["cn_x"].rearrange("(t p) k -> p t k", p=128)
    snv = cst["sn_x"].rearrange("(t p) k -> p t k", p=128)
    it = 0
    for kb in range(4):
        for h in range(2):
            bb = it % 2
            it += 1
            P.dma("sp", lambda e, bb=bb, h=h, kb=kb: e.dma_start(out=tb[bb][0][:], in_=cnv[:, 8 * h:8 * h + 8, kb * 512:(kb + 1) * 512]),
                  writes=[("tb", bb, 0)])
            P.dma("sp", lambda e, bb=bb, h=h, kb=kb: e.dma_start(out=tb[bb][1][:], in_=snv[:, 8 * h:8 * h + 8, kb * 512:(kb + 1) * 512]),
                  writes=[("tb", bb, 1)])
            for c in range(2):
                for tt in range(8):
                    t = 8 * h + tt
                    for j in range(2):
                        first = (h == 0 and tt == 0 and j == 0)
                        lastmm = (h == 1 and tt == 7 and j == 1)
                        P.op("pe", lambda e, bb=bb, c=c, t=t, tt=tt, j=j, first=first, lastmm=lastmm:
                             e.matmul(pf[c][:], lhsT=g[:, t, j * 256 + c * 128: j * 256 + (c + 1) * 128], rhs=tb[bb][j][:, tt, :],
                                      start=first, stop=lastmm),
                             reads=[("g", t), ("tb", bb, j)], writes=[("pf", c)])
        for c in range(2):
            P.op("act", lambda e, c=c, kb=kb: e.activation(out=frT[:, c, kb * 512:(kb + 1) * 512], in_=pf[c][:], func=AF.Copy),
                 reads=[("pf", c)], writes=["frT"])
    if not last:
        for c in range(2):
            for tt in range(2):
                for j in range(2):
                    tabl = C["cn_c"] if j == 0 else C["sn_c"]
                    P.op("pe", lambda e, c=c, tt=tt, j=j, tabl=tabl:
                         e.matmul(pf[c][:, :LC], lhsT=g[:, 16 + tt, j * 256 + c * 128: j * 256 + (c + 1) * 128], rhs=tabl[:, tt, :],
                                  start=(tt == 0 and j == 0), stop=(tt == 1 and j == 1)),
                         reads=[("g", 16 + tt)], writes=[("pf", c)])
            P.op("act", lambda e, c=c: e.activation(out=frT[:, c, L:NT], in_=pf[c][:, :LC], func=AF.Copy), reads=[("pf", c)], writes=["frT"])
    it = 0
    for co in range(2):
        for (t0, n) in (TB[:4] if last else TB):
            b = it % 2
            it += 1
            for ci in range(2):
                P.op("pe", lambda e, b=b, co=co, ci=ci, t0=t0, n=n: e.matmul(po[b][:, :n], lhsT=wf[:, ci, co * 128:(co + 1) * 128],
                                                                            rhs=frT[:, ci, t0:t0 + n], start=(ci == 0), stop=(ci == 1)),
                     reads=["wf", "frT"], writes=[("po", b)])
            P.op("act", lambda e, b=b, co=co, t0=t0, n=n: e.activation(out=catT[:, co, t0:t0 + n], in_=po[b][:, :n], func=AF.Copy),
                 reads=[("po", b)], writes=[("catT", co)])


def emit_pool(nc, P, st, sb, ps, l, C, w_p, pscale, pT, catT, last):
    W = L + 16
    A = sb("pA", [128, W], F32, st)
    S = [sb("pS%d" % i, [128, W], F32, st) for i in range(4)]
    dT = sb("dT", [128, 2, NT], BF16, st)
    wst = sb("wst", [128, 2, 128], F32, st)
    wbd = sb("wbd", [128, 2, 128], BF16, st)
    psc = sb("psc", [128, 2], F32, st)
    tmpb = sb("tmpb", [128, 8], F32, st)
    ppo = [ps("ppo%d" % i, [128, 512], F32, st) for i in range(2)]
    P.op("pool", lambda e: e.memset(wst[:], 0.0), writes=["wst"])
    for gi in range(4):
        c, hf = gi // 2, gi % 2
        P.dma("sp", lambda e, gi=gi, c=c, hf=hf: e.dma_start(out=wst[hf * 64:(hf + 1) * 64, c, hf * 64:(hf + 1) * 64], in_=w_p[l, gi]),
              reads=["wst"], writes=["wst"])
    P.op("dve", lambda e: e.tensor_copy(out=wbd[:], in_=wst[:]), reads=["wst"], writes=["wbd"])
    P.dma("sp", lambda e: e.dma_start(out=psc[:], in_=pscale[l]), writes=["psc"])
    shifts = [(-1, 0), (-1, 1), (-2, 2), (-4, 4)]
    ext = [7, 6, 4, 0]
    segs = [(0, L)] if last else [(0, L), (L, LC)]
    for (s0, n) in segs:
        for c in range(2):
            P.op("pool", lambda e: e.memset(A[:, 0:8], 0.0), writes=["A"])
            P.op("pool", lambda e, n=n: e.memset(A[:, 8 + n:16 + n], 0.0), writes=["A"])
            P.op("act", lambda e, c=c, s0=s0, n=n: e.activation(out=A[:, 8:8 + n], in_=pT[:, c, s0:s0 + n], func=AF.Copy), reads=["pT"], writes=["A"])
            nst = 2 if c == 0 else 4
            for si in range(nst):
                src = A if si == 0 else S[si - 1]
                sname = "A" if si == 0 else ("S", si - 1)
                lo = 8 - ext[si]
                hi = 8 + n + ext[si]
                a, bsh = shifts[si]
                eng = "dve" if si % 2 == 0 else "pool"
                P.op(eng, lambda e, si=si, src=src, lo=lo, hi=hi, a=a, bsh=bsh: e.tensor_tensor(out=S[si][:, lo:hi], in0=src[:, lo + a:hi + a],
                                                                                            in1=src[:, lo + bsh:hi + bsh], op=ALU.add),
                     reads=[sname], writes=[("S", si)])
            for hf in range(2):
                si = (0, 1)[hf] if c == 0 else (2, 3)[hf]
                pr = slice(hf * 64, (hf + 1) * 64)
                P.op("dve", lambda e, si=si, pr=pr, c=c, s0=s0, n=n: e.scalar_tensor_tensor(out=dT[pr, c, s0:s0 + n], in0=S[si][pr, 8:8 + n],
                                                                                        scalar=C["invw"][pr, c:c + 1], in1=A[pr, 8:8 + n],
                                                                                        op0=ALU.mult, op1=ALU.subtract),
                     reads=[("S", si), "A"], writes=["dT"])
                for (bo, col) in ((0, 8), (8, 8 + n - 8)):
                    P.op("dve", lambda e, si=si, pr=pr, c=c, bo=bo, col=col: e.tensor_tensor(out=tmpb[pr, :], in0=S[si][pr, col:col + 8],
                                                                                         in1=C["invb"][pr, c, bo:bo + 8], op=ALU.mult),
                         reads=[("S", si)], writes=["tmpb"])
                    P.op("dve", lambda e, pr=pr, c=c, s0=s0, col=col: e.tensor_tensor(out=dT[pr, c, s0 + col - 8:s0 + col], in0=tmpb[pr, :],
                                                                                  in1=A[pr, col:col + 8], op=ALU.subtract),
                         reads=["tmpb", "A"], writes=["dT"])
    it = 0
    for c in range(2):
        for (t0, n) in (TB[:4] if last else TB):
            b = it % 2
            it += 1
            P.op("pe", lambda e, b=b, c=c, t0=t0, n=n: e.matmul(ppo[b][:, :n], lhsT=wbd[:, c, :], rhs=dT[:, c, t0:t0 + n], start=True, stop=True),
                 reads=["wbd", "dT"], writes=[("ppo", b)])
            P.op("act", lambda e, b=b, c=c, t0=t0, n=n: e.activation(out=catT[:, 2 + c, t0:t0 + n], in_=ppo[b][:, :n], func=AF.Copy,
                                                                    scale=psc[:, c:c + 1]),
                 reads=[("ppo", b), "psc"], writes=[("catT", 2 + c)])


def emit_attn(nc, P, st, sb, ps, l, C, qT, kT, v, catT, last):
    NB = 4
    pe_ = [sb("pexp%d" % i, [128, 512], BF16, st) for i in range(NB)]
    rden = [sb("rden%d" % i, [128, 512], F32, st) for i in range(2)]
    psc = [ps("psc%d" % i, [128, 512], F32, st) for i in range(NB)]
    po = [ps("pao%d" % i, [128, 512], F32, st) for i in range(2)]
    pden = [ps("pden%d" % i, [128, 512], F32, st) for i in range(2)]
    scale = 1.0 / math.sqrt(128.0)
    jobs = []
    for h in range(4):
        for qb in range(4):
            jobs.append((h, qb * 512, 512, list(range(NTILE))))
        if not last:
            jobs.append((h, L, LC, [16, 17]))
    items = []
    for ji, (h, q0, n, ktiles) in enumerate(jobs):
        for i, kt in enumerate(ktiles):
            items.append((ji, h, q0, n, kt, i, len(ktiles)))

    def s0(ii):
        ji, h, q0, n, kt, i, nk = items[ii]
        b = ii % NB
        kvh = h // 2
        P.op("pe", lambda e, b=b, kvh=kvh, kt=kt, h=h, q0=q0, n=n: e.matmul(psc[b][:, :n], lhsT=kT[:, kvh, kt * 128:(kt + 1) * 128],
                                                                           rhs=qT[:, h, q0:q0 + n], start=True, stop=True),
             reads=["qT", "kT"], writes=[("psc", b)])
        P.op("act", lambda e, b=b, n=n: e.activation(out=pe_[b][:, :n], in_=psc[b][:, :n], func=AF.Exp, scale=scale, bias=-SM_SHIFT),
             reads=[("psc", b)], writes=[("pexp", b)])

    def s1(ii):
        ji, h, q0, n, kt, i, nk = items[ii]
        b = ii % NB
        ob = ji % 2
        kvh = h // 2
        P.op("pe", lambda e, b=b, ob=ob, kt=kt, kvh=kvh, n=n, i=i, nk=nk: e.matmul(po[ob][:, :n], lhsT=v[:, kt, kvh * 128:(kvh + 1) * 128],
                                                                               rhs=pe_[b][:, :n], start=(i == 0), stop=(i == nk - 1)),
             reads=["v", ("pexp", b)], writes=[("po", ob)])
        P.op("pe", lambda e, b=b, ob=ob, n=n, i=i, nk=nk: e.matmul(pden[ob][:, :n], lhsT=C["ones_bf"][:], rhs=pe_[b][:, :n],
                                                                  start=(i == 0), stop=(i == nk - 1)),
             reads=[("pexp", b)], writes=[("pden", ob)])
        if i == nk - 1:
            P.op("dve", lambda e, ob=ob, n=n: e.reciprocal(out=rden[ob][:, :n], in_=pden[ob][:, :n]), reads=[("pden", ob)], writes=[("rden", ob)])
            P.op("dve", lambda e, ob=ob, n=n, h=h, q0=q0: e.tensor_tensor(out=catT[:, 4 + h, q0:q0 + n], in0=po[ob][:, :n], in1=rden[ob][:, :n], op=ALU.mult),
                 reads=[("po", ob), ("rden", ob)], writes=[("catT", 4 + h)])

    run_pipeline(len(items), [s0, s1], [0, 2])


def emit_outproj(nc, P, st, sb, ps, l, C, X, MOD, n2g, w_out, catT, h2T, H2, last, xsrc=None):
    wo = sb("wo", [128, 8, D], BF16, st)
    wog = [sb("wog%d" % r, [128, 8, D], BF16, st) for r in range(2)]
    g1 = [sb("g1_%d" % r, [128, D], F32, st) for r in range(2)]
    xin = [sb("xin%d" % i, [128, D], F32, st) for i in range(3)]
    pso = [ps("pso%d" % i, [128, 2, 512], F32, st) for i in range(3)]
    wv = w_out[l].rearrange("(k p) n -> p k n", p=128)
    for h in range(4):
        P.dma("pool", lambda e, h=h: e.dma_start(out=wo[:, 2 * h:2 * h + 2, :], in_=wv[:, 2 * h:2 * h + 2, :]), writes=[("wo", h)])
    nr = 1 if last else 2
    for r in range(nr):
        P.dma("sp", lambda e, r=r: e.dma_start(out=g1[r][:], in_=bcast_rows(MOD[l, r, 2 * D:3 * D])), writes=[("g1", r)])
    for k in range(8):
        for r in range(nr):
            eng = "dve" if r == 0 else "pool"
            P.op(eng, lambda e, k=k, r=r: e.tensor_tensor(out=wog[r][:, k, :], in0=wo[:, k, :], in1=g1[r][:], op=ALU.mult),
                 reads=[("wo", k // 2), ("g1", r)], writes=[("wog", r, k)])
    ntl = 16 if last else NTILE

    def pre_fn(t):
        b = t % 3
        src = xsrc(t) if xsrc else X[t * 128:(t + 1) * 128, :]
        P.dma("sp", lambda e, b=b, src=src: e.dma_start(out=xin[b][:], in_=src), reads=[("X", t)], writes=[("xin", b)])

    def load_fn(t, b, xt_b):
        r = 0 if t < 16 else 1
        for half in range(2):
            for j in range(8):
                P.op("pe", lambda e, b=b, half=half, j=j, t=t, r=r: e.matmul(pso[b][:, half, :], lhsT=catT[:, j, t * 128:(t + 1) * 128],
                                                                            rhs=wog[r][:, j, half * 512:(half + 1) * 512], start=(j == 0), stop=(j == 7)),
                     reads=[("wog", r, j), ("bigA", t)], writes=[("pso", b, half)])
        P.op("dve", lambda e, b=b, xt_b=xt_b: e.tensor_tensor(out=xt_b[:].rearrange("p (a n) -> p a n", a=2), in0=pso[b][:],
                                                             in1=xin[b][:].rearrange("p (a n) -> p a n", a=2), op=ALU.add),
             reads=[("pso", b, 0), ("pso", b, 1), ("xin", b)], writes=[("xt", b)])
        P.dma("sp", lambda e, b=b, t=t, xt_b=xt_b: e.dma_start(out=X[t * 128:(t + 1) * 128, :], in_=xt_b[:]), reads=[("xt", b)], writes=[("X", t)])

    emit_modulate(nc, P, st, sb, ps, l, X, MOD, n2g, C["ident_bf"], h2T, 1, H2, ntl, load_fn=load_fn, dst_key=lambda t: ("bigA", t),
                  pre_fn=pre_fn)


def emit_router_logits(nc, P, st, sb, ps, l, C, w_r, h2T, AFF, last):
    ntl = 16 if last else NTILE
    wr = sb("wr", [128, 8, NE], BF16, st)
    mx = sb("rmx", [128, NTILE], F32, st)
    se = sb("rse", [128, NTILE], F32, st)
    ex = sb("rex", [128, NTILE, NE], F32, st)
    plog = ps("plog", [128, NTILE, NE], F32, st)
    P.dma("pool", lambda e: e.dma_start(out=wr[:], in_=w_r[l].rearrange("(k p) n -> p k n", p=128)), writes=["wr"])
    for t in range(ntl):
        for k in range(8):
            P.op("pe", lambda e, t=t, k=k: e.matmul(plog[:, t, :], lhsT=h2T[:, k, t * 128:(t + 1) * 128], rhs=wr[:, k, :],
                                                  start=(k == 0), stop=(k == 7)),
                 reads=["wr", "h2T"], writes=["plog"])
    P.op("dve", lambda e: e.tensor_reduce(out=mx[:, :ntl], in_=plog[:, :ntl, :], axis=AX.X, op=ALU.max), reads=["plog"], writes=["mx"])
    P.op("dve", lambda e: e.tensor_scalar(out=mx[:, :ntl], in0=mx[:, :ntl], scalar1=-1.0, scalar2=None, op0=ALU.mult), reads=["mx"], writes=["mx"])
    for t in range(ntl):
        P.op("act", lambda e, t=t: e.activation(out=ex[:, t, :], in_=plog[:, t, :], func=AF.Exp, bias=mx[:, t:t + 1], scale=1.0,
                                                accum_out=se[:, t:t + 1]),
             reads=["plog", "mx"], writes=[("ex", t), ("se", t)])
    P.op("dve", lambda e: e.reciprocal(out=se[:, :ntl], in_=se[:, :ntl]), reads=[("se", t) for t in range(ntl)], writes=["rse"])
    for t in range(ntl):
        P.op("dve", lambda e, t=t: e.tensor_scalar(out=AFF[:, t, :], in0=ex[:, t, :], scalar1=se[:, t:t + 1], scalar2=None, op0=ALU.mult),
             reads=[("ex", t), "rse"], writes=[("aff", t)])


def emit_topk(nc, P, st, sb, ps, l, C, AFF, IDXP, GP, last):
    ntl = 16 if last else NTILE
    affT = sb("affT", [NE, NT], F32, st)
    work = sb("rwork", [NE, L], F32, st)
    workc = sb("rworkc", [NE, LC], F32, st)
    vals = sb("rvals", [NE, CAPT], F32, st)
    idxu = sb("ridxu", [NE, CAPT], U32, st)
    idxf = sb("ridxf", [NE, CAPT], F32, st)
    prt = ps("prt", [128, 512], F32, st)
    for bk in range((ntl + 3) // 4):
        nn = min(4, ntl - bk * 4)
        for q in range(nn):
            t = bk * 4 + q
            P.op("pe", lambda e, t=t, q=q: e.transpose(out=prt[:NE, q * 128:(q + 1) * 128], in_=AFF[:, t, :], identity=C["ident_f"][:]),
                 reads=[("aff", t)], writes=["prt"])
        P.op("act", lambda e, bk=bk, nn=nn: e.activation(out=affT[:, bk * 512:bk * 512 + nn * 128], in_=prt[:NE, :nn * 128], func=AF.Copy),
             reads=["prt"], writes=["affT"])
    P.op("dve", lambda e: e.tensor_copy(out=work[:], in_=affT[:, :L]), reads=["affT"], writes=["work"])
    for r in range(CAPX // 8):
        sl = slice(r * 8, (r + 1) * 8)
        P.op("dve", lambda e, sl=sl: e.max(out=vals[:, sl], in_=work[:]), reads=["work"], writes=["vals"])
        P.op("dve", lambda e, sl=sl: e.max_index(out=idxu[:, sl], in_max=vals[:, sl], in_values=work[:]), reads=["work", "vals"], writes=["idxu"])
        P.op("dve", lambda e, sl=sl: e.match_replace(out=work[:], in_to_replace=vals[:, sl], in_values=work[:], imm_value=-1.0),
             reads=["work", "vals"], writes=["work"])
    P.op("dve", lambda e: e.tensor_copy(out=idxf[:, :CAPX], in_=idxu[:, :CAPX]), reads=["idxu"], writes=["idxf"])
    if not last:
        P.op("dve", lambda e: e.tensor_copy(out=workc[:], in_=affT[:, L:NT]), reads=["affT"], writes=["workc"])
        for r in range(CAPC // 8):
            sl = slice(CAPX + r * 8, CAPX + (r + 1) * 8)
            P.op("dve", lambda e, sl=sl: e.max(out=vals[:, sl], in_=workc[:]), reads=["workc"], writes=["vals"])
            P.op("dve", lambda e, sl=sl: e.max_index(out=idxu[:, sl], in_max=vals[:, sl], in_values=workc[:]), reads=["workc", "vals"], writes=["idxu"])
            P.op("dve", lambda e, sl=sl: e.match_replace(out=workc[:], in_to_replace=vals[:, sl], in_values=workc[:], imm_value=-1.0),
                 reads=["workc", "vals"], writes=["workc"])
        P.op("dve", lambda e: e.tensor_copy(out=idxf[:, CAPX:], in_=idxu[:, CAPX:]), reads=["idxu"], writes=["idxf"])
        P.op("dve", lambda e: e.tensor_scalar(out=idxf[:, CAPX:], in0=idxf[:, CAPX:], scalar1=float(L), scalar2=None, op0=ALU.add),
             reads=["idxf"], writes=["idxf"])
    rows = [128, 128, 32]
    for j in range(2 if last else 3):
        c0 = j * 128
        for q, src, dstt, nm in ((0, idxf, IDXP, "IDXP"), (1, vals, GP, "GP")):
            P.op("pe", lambda e, j=j, c0=c0, q=q, src=src: e.transpose(out=prt[:rows[j], q * NE:(q + 1) * NE], in_=src[:, c0:c0 + rows[j]],
                                                                      identity=C["ident_f"][:NE, :NE]),
                 reads=["idxf", "vals", "affT"], writes=["prt"])
            P.op("dve", lambda e, j=j, q=q, dstt=dstt: e.tensor_copy(out=dstt[:rows[j], :, j], in_=prt[:rows[j], q * NE:(q + 1) * NE]),
                 reads=["prt"], writes=[nm])


def emit_experts(nc, P, st, sb, ps, l, C, MOD, w_g, w_u, w_d, H2, X, IDXP, GP, last, AFF):
    g2 = [sb("g2_%d" % r, [128, D], F32, st) for r in range(2)]
    W = [[sb("W%d_%d" % (i, j), [128, 8, D], BF16, st) for j in range(3)] for i in range(2)]
    xs = [sb("xs%d" % i, [128, 3, D], BF16, st) for i in range(2)]
    xsT = [sb("xsT%d" % i, [128, 8, CAPT], BF16, st) for i in range(2)]
    sa = [sb("sa%d" % i, [128, CAPT], F32, st) for i in range(2)]
    hh = sb("hh", [128, 8, CAPT], BF16, st)
    yo = [[sb("yo%d_%d" % (q, i), [128, D], F32, st) for i in range(3)] for q in range(2)]
    ptx = ps("ptx", [128, 8, 128], BF16, st)
    sc_ev = {}
    pa = [ps("pa%d" % i, [128, 512], F32, st) for i in range(2)]
    pu = [ps("pu%d" % i, [128, 512], F32, st) for i in range(2)]
    py = [ps("py%d" % i, [128, 512], F32, st) for i in range(2)]
    nj = 2 if last else 3
    ncol = CAPX if last else CAPT
    rows = [128, 128, 32]
    srcs = (w_g, w_u, w_d)
    cnt = {"fi": 0, "pi": 0}

    def load_w(ex):
        wb = ex % 2
        for i in range(3):
            wv = srcs[i][l, ex].rearrange("(k p) n -> p k n", p=128)
            for h in range(2):
                P.dma("pool", lambda e, wb=wb, i=i, h=h, wv=wv: e.dma_start(out=W[wb][i][:, 4 * h:4 * h + 4, :], in_=wv[:, 4 * h:4 * h + 4, :]),
                      writes=[("W", wb, i, h)])

    def gather(ex):
        wb = ex % 2
        for j in range(nj):
            P.dma("pool", lambda e, wb=wb, j=j, ex=ex: e.indirect_dma_start(out=xs[wb][:rows[j], j, :], out_offset=None,
                                                                            in_=H2[0:(L if last else NT), :],
                                                                            in_offset=IOA(ap=IDXP[:rows[j], ex, j:j + 1], axis=0)),
                  reads=["H2", "IDXP"], writes=[("xs", wb, j)])

    def transposes(ex):
        wb = ex % 2
        for j in range(nj):
            for k in range(8):
                P.op("pe", lambda e, wb=wb, j=j, k=k: e.transpose(out=ptx[:, k, :rows[j]], in_=xs[wb][:rows[j], j, k * 128:(k + 1) * 128],
                                                                 identity=C["ident_bf"][:rows[j], :rows[j]]),
                     reads=[("xs", wb, j)], writes=["ptx"])
            P.op("act", lambda e, wb=wb, j=j: e.activation(out=xsT[wb][:, :, j * 128:j * 128 + rows[j]], in_=ptx[:, :, :rows[j]], func=AF.Copy),
                 reads=["ptx"], writes=[("xsT", wb)])

    def gate_up(ex):
        wb = ex % 2
        for fc in range(8):
            fb = cnt["fi"] % 2
            cnt["fi"] += 1
            for k in range(8):
                P.op("pe", lambda e, wb=wb, fb=fb, fc=fc, k=k: e.matmul(pa[fb][:, :ncol], lhsT=W[wb][0][:, k, fc * 128:(fc + 1) * 128],
                                                                       rhs=xsT[wb][:, k, :ncol], start=(k == 0), stop=(k == 7)),
                     reads=[("W", wb, 0, k // 4), ("xsT", wb)], writes=[("pa", fb)])
            for k in range(8):
                P.op("pe", lambda e, wb=wb, fb=fb, fc=fc, k=k: e.matmul(pu[fb][:, :ncol], lhsT=W[wb][1][:, k, fc * 128:(fc + 1) * 128],
                                                                       rhs=xsT[wb][:, k, :ncol], start=(k == 0), stop=(k == 7)),
                     reads=[("W", wb, 1, k // 4), ("xsT", wb)], writes=[("pu", fb)])
            P.op("act", lambda e, fb=fb: e.activation(out=sa[fb][:, :ncol], in_=pa[fb][:, :ncol], func=AF.Silu), reads=[("pa", fb)], writes=[("sa", fb)])
            P.op("dve", lambda e, fb=fb, fc=fc: e.tensor_tensor(out=hh[:, fc, :ncol], in0=sa[fb][:, :ncol], in1=pu[fb][:, :ncol], op=ALU.mult),
                 reads=[("sa", fb), ("pu", fb)], writes=[("hh", fc)])

    def down(ex):
        wb = ex % 2
        for j in range(nj):
            r = 0 if j < 2 else 1
            rw = rows[j]
            for half in range(2):
                pb = cnt["pi"] % 2
                cnt["pi"] += 1
                for fc in range(8):
                    P.op("pe", lambda e, wb=wb, pb=pb, fc=fc, j=j, rw=rw, half=half: e.matmul(py[pb][:rw, :], lhsT=hh[:, fc, j * 128:j * 128 + rw],
                                                                                           rhs=W[wb][2][:, fc, half * 512:(half + 1) * 512],
                                                                                           start=(fc == 0), stop=(fc == 7)),
                         reads=[("W", wb, 2, fc // 4), ("hh", fc)], writes=[("py", pb)])
                P.op("dve", lambda e, pb=pb, j=j, rw=rw, half=half, ex=ex, r=r, wb=wb: e.scalar_tensor_tensor(
                    out=yo[wb][j][:rw, half * 512:(half + 1) * 512], in0=py[pb][:rw, :], scalar=GP[:rw, ex, j:j + 1],
                    in1=g2[r][:rw, half * 512:(half + 1) * 512], op0=ALU.mult, op1=ALU.mult),
                    reads=[("py", pb), "GP", ("g2", r)], writes=[("yo", wb, j, half)])

    def scatter(ex):
        wb = ex % 2
        prev = sc_ev.get(ex - 1, [])
        evs = []
        for j in range(nj):
            rw = rows[j]
            evs.append(P.dma("pool", lambda e, j=j, rw=rw, ex=ex, wb=wb: e.indirect_dma_start(
                out=X, out_offset=IOA(ap=IDXP[:rw, ex, j:j + 1], axis=0), in_=yo[wb][j][:rw, :], in_offset=None, compute_op=ALU.add),
                reads=[("yo", wb, j, 0), ("yo", wb, j, 1), "IDXP"], writes=[("Xsc", ex, j)], after=prev))
        sc_ev[ex] = evs

    load_w(0)
    load_w(1)
    emit_topk(nc, P, st, sb, ps, l, C, AFF, IDXP, GP, last)
    gather(0)
    gather(1)
    for r in range(2):
        P.dma("sp", lambda e, r=r: e.dma_start(out=g2[r][:], in_=bcast_rows(MOD[l, r, 5 * D:6 * D])), writes=[("g2", r)])
    transposes(0)
    for ex in range(NE):
        gate_up(ex)
        if ex + 1 < NE:
            transposes(ex + 1)
        down(ex)
        scatter(ex)
        if ex + 2 < NE:
            gather(ex + 2)
            load_w(ex + 2)


def kernel(**inputs):
    depth = inputs["ada_w"].shape[0]
    nb = inputs["x"].shape[0]
    nc, _ = build_program(depth)
    f32 = lambda a: np.ascontiguousarray(np.asarray(a, dtype=np.float32))
    shared = {}
    for nm in ["ada_w", "ada_b", "norm1_g", "norm2_g", "w_in", "w_fourier", "w_pool", "q_norm_g", "k_norm_g",
               "w_out", "w_router", "w_gate", "w_up", "w_down"]:
        shared[nm] = f32(inputs[nm])
    shared["pool_scale"] = np.ascontiguousarray(f32(inputs["pool_scale"]).reshape(depth, 2, 128).transpose(0, 2, 1))
    for k, v in make_consts().items():
        shared["k_" + k] = v
    x = f32(inputs["x"])
    ctx = f32(inputs["ctx"])
    c = f32(inputs["c"])
    c_ctx = f32(inputs["c_ctx"])
    in_maps = []
    for b in range(nb):
        m = dict(shared)
        m["x"] = np.ascontiguousarray(x[b])
        m["ctx"] = np.ascontiguousarray(ctx[b])
        cT = np.stack([c[b], c_ctx], axis=-1)
        m["cT"] = np.ascontiguousarray(cT.reshape(8, 128, 2).transpose(1, 0, 2))
        in_maps.append(m)
    res = run_bass_kernel_spmd(nc, in_maps, core_ids=list(range(nb)))
    return np.stack([np.asarray(r["out"], dtype=np.float32) for r in res.results], axis=0)
```
